# Optimizing a Trainium2 kernel written in Bass

```python
import jax, jax.numpy as jnp
from jax import lax
import numpy as np

D_MODEL = 2048
BATCH = 4
SEQ = 2048
DEPTH = 2

POOL_WIDTH = D_MODEL // 4
POOL_WINDOWS = (2, 4, 8, 16)
POOL_GROUPS = len(POOL_WINDOWS)
POOL_GROUP_DIM = POOL_WIDTH // POOL_GROUPS
SB_HEAD_DIM = 128
SB_WIDTH = D_MODEL // 2
SB_HEADS = SB_WIDTH // SB_HEAD_DIM
Q_BLOCK = 128
HG_HEAD_DIM = 128
HG_WIDTH = D_MODEL // 4
HG_HEADS = HG_WIDTH // HG_HEAD_DIM
HG_CHUNK = 64
LB_FLOOR = 1e-20
N_BRANCH = 3
D_FF = -(-8 * D_MODEL // (3 * 256)) * 256
PLE_DIM = 256
EPS = 1e-6

IN_SPLITS = (POOL_WIDTH, SB_WIDTH, SB_WIDTH, SB_WIDTH,
             HG_WIDTH, HG_WIDTH, HG_WIDTH, HG_WIDTH, N_BRANCH * D_MODEL)
IN_COLS = sum(IN_SPLITS)

kernel_name = "pool_stickbreak_hgrn2_gated_hybrid"


def rms_norm(x, g):
    xf = x.astype(jnp.float32)
    y = xf * lax.rsqrt(jnp.mean(xf * xf, axis=-1, keepdims=True) + EPS)
    return (y * g.astype(jnp.float32)).astype(x.dtype)


def pool_mixer(u, w_group, scale):
    B, S, _ = u.shape
    ug = u.reshape(B, S, POOL_GROUPS, POOL_GROUP_DIM).astype(jnp.float32)
    c = jnp.cumsum(ug, axis=1)
    pos = jnp.arange(S)
    means = []
    for gi, w in enumerate(POOL_WINDOWS):
        cp = jnp.pad(c[:, :, gi], ((0, 0), (w, 0), (0, 0)))
        win_sum = cp[:, w:w + S] - cp[:, :S]
        cnt = jnp.minimum(pos + 1, w).astype(jnp.float32)
        means.append(win_sum / cnt[None, :, None])
    mean = jnp.stack(means, axis=2)
    mixed = (mean - ug).astype(u.dtype)
    y = jnp.einsum('bsgc,gcd->bsgd', mixed, w_group)
    return y.reshape(B, S, POOL_WIDTH) * scale


def stick_breaking_attention(q, k, v):
    S = q.shape[2]
    scale = SB_HEAD_DIM ** -0.5
    outs = []
    for blk in range(S // Q_BLOCK):
        q0 = blk * Q_BLOCK
        kend = q0 + Q_BLOCK
        qb = q[:, :, q0:kend]
        kb = k[:, :, :kend]
        vb = v[:, :, :kend]
        z = jnp.einsum('bhtd,bhsd->bhts', qb, kb).astype(jnp.float32) * scale
        tpos = q0 + jnp.arange(Q_BLOCK)
        spos = jnp.arange(kend)
        mask = spos[None, :] < tpos[:, None]
        log_1mb = jnp.where(mask, jax.nn.log_sigmoid(-z), 0.0)
        later = lax.cumsum(log_1mb, axis=3, reverse=True) - log_1mb
        a = jnp.where(mask, jnp.exp(jax.nn.log_sigmoid(z) + later), 0.0)
        outs.append(jnp.einsum('bhts,bhsd->bhtd', a.astype(v.dtype), vb))
    return jnp.concatenate(outs, axis=2)


def hgrn2_recurrence(q, k, v, log_f):
    B, H, S, Dk = q.shape
    Dv = v.shape[-1]
    n = S // HG_CHUNK

    def to_chunks(a):
        return a.reshape(B, H, n, HG_CHUNK, a.shape[-1]).transpose(2, 0, 1, 3, 4)

    qc, kc, vc, fc = to_chunks(q), to_chunks(k), to_chunks(v), to_chunks(log_f)
    causal = jnp.tril(jnp.ones((HG_CHUNK, HG_CHUNK), dtype=bool))[:, :, None]

    def step(state, inp):
        qi, ki, vi, fi = inp
        b = jnp.cumsum(fi, axis=2)
        o_inter = jnp.einsum('bhtk,bhkv->bhtv', qi * jnp.exp(b), state)
        diff = b[:, :, :, None, :] - b[:, :, None, :, :]
        decay = jnp.where(causal, jnp.exp(jnp.minimum(diff, 0.0)), 0.0)
        scores = jnp.einsum('bhtk,bhsk,bhtsk->bhts', qi, ki, decay)
        o_intra = jnp.einsum('bhts,bhsv->bhtv', scores, vi)
        b_last = b[:, :, -1:, :]
        k_dec = ki * jnp.exp(b_last - b)
        new_state = state * jnp.exp(b_last[:, :, 0, :, None]) + jnp.einsum('bhsk,bhsv->bhkv', k_dec, vi)
        return new_state, o_inter + o_intra

    state0 = jnp.zeros((B, H, Dk, Dv), jnp.float32)
    _, ys = lax.scan(step, state0, (qc, kc, vc, fc))
    return ys.transpose(1, 2, 0, 3, 4).reshape(B, H, S, Dv)


def setup_inputs(seed: int = 0) -> dict:
    key = jax.random.key(seed)
    ks = jax.random.split(key, 20)
    f32 = jnp.float32

    def dense(k, shape, fan_in):
        return jax.random.normal(k, shape, f32) * (fan_in ** -0.5)

    def gain(k, shape):
        return 1.0 + 0.02 * jax.random.normal(k, shape, f32)

    return {
        "x": jax.random.normal(ks[0], (BATCH, SEQ, D_MODEL), f32),
        "p": jax.random.normal(ks[1], (DEPTH, BATCH, SEQ, PLE_DIM), f32),
        "norm_mix": gain(ks[2], (DEPTH, D_MODEL)),
        "w_in": dense(ks[3], (DEPTH, D_MODEL, IN_COLS), D_MODEL),
        "pool_w": dense(ks[4], (DEPTH, POOL_GROUPS, POOL_GROUP_DIM, POOL_GROUP_DIM), POOL_GROUP_DIM),
        "pool_scale": gain(ks[5], (DEPTH, POOL_WIDTH)),
        "hg_lb": 0.1 * jax.random.normal(ks[6], (DEPTH, HG_WIDTH), f32),
        "hg_norm": gain(ks[7], (DEPTH, HG_WIDTH)),
        "w_br_pool": dense(ks[8], (DEPTH, POOL_WIDTH, D_MODEL), POOL_WIDTH),
        "w_br_sb": dense(ks[9], (DEPTH, SB_WIDTH, D_MODEL), SB_WIDTH),
        "w_br_hg": dense(ks[10], (DEPTH, HG_WIDTH, D_MODEL), HG_WIDTH),
        "w_out": dense(ks[11], (DEPTH, D_MODEL, D_MODEL), D_MODEL),
        "norm_ffn": gain(ks[12], (DEPTH, D_MODEL)),
        "w_gate_up": dense(ks[13], (DEPTH, D_MODEL, 2 * D_FF), D_MODEL),
        "w_down": dense(ks[14], (DEPTH, D_FF, D_MODEL), D_FF),
        "norm_ple": gain(ks[15], (DEPTH, D_MODEL)),
        "w_ple_gate": dense(ks[16], (DEPTH, D_MODEL, D_MODEL), D_MODEL),
        "w_ple_proj": dense(ks[17], (DEPTH, PLE_DIM, D_MODEL), PLE_DIM),
        "norm_final": gain(ks[18], (D_MODEL,)),
    }


def reference(x, p, norm_mix, w_in, pool_w, pool_scale, hg_lb, hg_norm, w_br_pool, w_br_sb, w_br_hg,
              w_out, norm_ffn, w_gate_up, w_down, norm_ple, w_ple_gate, w_ple_proj, norm_final):
    B, S, _ = x.shape
    split_points = [int(c) for c in np.cumsum(IN_SPLITS)[:-1]]

    def heads(t, n):
        return t.reshape(B, S, n, -1).transpose(0, 2, 1, 3)

    def merge(t):
        return t.transpose(0, 2, 1, 3).reshape(B, S, -1)

    lb_sm = jax.nn.softmax(hg_lb.astype(jnp.float32), axis=0)
    lower_bounds = jnp.cumsum(lb_sm, axis=0) - lb_sm[0:1]

    for i in range(DEPTH):
        h = rms_norm(x, norm_mix[i])
        proj = h @ w_in[i]
        u_pool, sq, sk, sv, zf, hv, hq, og, gl = jnp.split(proj, split_points, axis=-1)

        y_pool = pool_mixer(u_pool, pool_w[i], pool_scale[i])

        y_sb = merge(stick_breaking_attention(heads(sq, SB_HEADS), heads(sk, SB_HEADS), heads(sv, SB_HEADS)))

        lb = jnp.clip(lower_bounds[i], 0.0, 1.0)
        zf32 = zf.astype(jnp.float32)
        log_f = jnp.logaddexp(jnp.log(jnp.maximum(lb, LB_FLOOR)),
                              jnp.log1p(-jnp.minimum(lb, 1.0 - 1e-6)) + jax.nn.log_sigmoid(zf32))
        k_in = (1.0 - lb) * jax.nn.sigmoid(-zf32)
        q_hg = jax.nn.silu(hq.astype(jnp.float32))
        o = hgrn2_recurrence(heads(q_hg, HG_HEADS), heads(k_in, HG_HEADS),
                             heads(hv.astype(jnp.float32), HG_HEADS), heads(log_f, HG_HEADS))
        o = o * lax.rsqrt(jnp.mean(o * o, axis=-1, keepdims=True) + EPS)
        y_hg = (merge(o) * hg_norm[i].astype(jnp.float32) * jax.nn.silu(og.astype(jnp.float32))).astype(x.dtype)

        gates = jax.nn.sigmoid(gl).reshape(B, S, N_BRANCH, D_MODEL)
        mixed = (gates[:, :, 0] * (y_pool @ w_br_pool[i])
                 + gates[:, :, 1] * (y_sb @ w_br_sb[i])
                 + gates[:, :, 2] * (y_hg @ w_br_hg[i]))
        x = x + mixed @ w_out[i]

        h = rms_norm(x, norm_ffn[i])
        g_ff, u_ff = jnp.split(h @ w_gate_up[i], 2, axis=-1)
        x = x + (jax.nn.silu(g_ff) * u_ff) @ w_down[i]

        ple_gate = jax.nn.sigmoid(rms_norm(x, norm_ple[i]) @ w_ple_gate[i])
        x = x + ple_gate * (p[i] @ w_ple_proj[i])

    return rms_norm(x, norm_final)
```

```python
import contextlib
import numpy as np
import ml_dtypes
import concourse.bass as bass
import concourse.mybir as mybir
from concourse.bass_utils import run_bass_kernel_spmd

F32 = mybir.dt.float32
BF16 = mybir.dt.bfloat16
ALU = mybir.AluOpType
AF = mybir.ActivationFunctionType

D = 2048
T = 1024
NKC = 16
DFF = 5632
EPS = 1e-6
C_POOL, C_SQ, C_SK, C_SV, C_ZF, C_HV, C_HQ, C_OG, C_GL = 0, 512, 1536, 2560, 3584, 4096, 4608, 5120, 5632
NEG = -30000.0
WNAMES = ["w_in", "pool_w", "w_br_pool", "w_br_sb", "w_br_hg", "w_out", "w_gate_up", "w_down",
          "w_ple_gate", "w_ple_proj"]


class Res:
    __slots__ = ("name", "w", "r")

    def __init__(self, name):
        self.name = name
        self.w = None
        self.r = []


def alias(new, old):
    toks = []
    for o in old:
        if o.w is not None:
            toks.append(o.w)
        toks.extend(o.r)
    for n in new:
        n.w = None
        n.r = list(toks)


def unalias(parents, subs):
    toks = []
    for o in subs:
        if o.w is not None:
            toks.append(o.w)
        toks.extend(o.r)
    for p in parents:
        p.r = list(p.r) + toks


class Prog:
    ENGS = ("pe", "act", "dve", "pool", "sp")

    def __init__(self, nc):
        self.nc = nc
        self.ops = {e: [] for e in self.ENGS}
        self.cnt = {e: 0 for e in self.ENGS}
        self.seen = {e: {} for e in self.ENGS}
        self.dsem_cnt = {}
        self.sem_names = set(self.ENGS)

    def _deps(self, eng, reads, writes):
        need = {}

        def add(tok):
            if tok is None:
                return
            s, v = tok
            if s == "pe" and eng == "pe":
                return
            if need.get(s, 0) < v:
                need[s] = v
        for r in reads:
            add(r.w)
        for w in writes:
            add(w.w)
            for t in w.r:
                add(t)
        waits = []
        seen = self.seen[eng]
        for s, v in need.items():
            if seen.get(s, 0) < v:
                seen[s] = v
                waits.append((s, v))
        return waits

    def _commit(self, tok, reads, writes):
        for r in reads:
            if len(r.r) > 64:
                mx = {}
                for s, v in r.r:
                    if mx.get(s, 0) < v:
                        mx[s] = v
                r.r = list(mx.items())
            r.r.append(tok)
        for w in writes:
            w.w = tok
            w.r = []

    def op(self, eng, fn, reads=(), writes=()):
        waits = self._deps(eng, reads, writes)
        self.cnt[eng] += 1
        tok = (eng, self.cnt[eng])
        self.ops[eng].append((waits, fn, (eng, 1)))
        self._commit(tok, reads, writes)
        return tok

    def dma(self, queue, sem, fns, reads=(), writes=()):
        self.sem_names.add(sem)
        prev = self.dsem_cnt.get(sem, 0)
        waits = self._deps(queue, reads, writes)
        if prev and self.seen[queue].get(sem, 0) < prev:
            self.seen[queue][sem] = prev
            waits.append((sem, prev))
        first = True
        for fn in fns:
            self.ops[queue].append((waits if first else [], fn, (sem, 16)))
            first = False
        tot = prev + 16 * len(fns)
        self.dsem_cnt[sem] = tot
        tok = (sem, tot)
        self._commit(tok, reads, writes)
        return tok

    def wait_tok(self, eng, tok):
        s, v = tok
        if self.seen[eng].get(s, 0) < v:
            self.seen[eng][s] = v
            self.ops[eng].append(([(s, v)], None, None))

    def emit(self):
        nc = self.nc
        with contextlib.ExitStack() as st:
            sems = {}
            for name in sorted(self.sem_names):
                sems[name] = st.enter_context(nc.semaphore("s_" + name))
            block = st.enter_context(nc.Block())

            def run(engname):
                def body(e):
                    for waits, fn, inc in self.ops[engname]:
                        for s, v in waits:
                            e.wait_ge(sems[s], v)
                        if fn is not None:
                            ins = fn(e)
                            if inc is not None:
                                ins.then_inc(sems[inc[0]], inc[1])
                return body
            block.tensor(run("pe"))
            block.scalar(run("act"))
            block.vector(run("dve"))
            block.gpsimd(run("pool"))
            block.sync(run("sp"))


class Tl:
    __slots__ = ("ap", "res")

    def __init__(self, ap, res):
        self.ap = ap
        self.res = res


def build(passes, final=True, NL=2, dbg=False):
    nc = bass.Bass("TRN2", target_bir_lowering=False)
    P = Prog(nc)
    dr = {}

    def din(name, shape, dt=F32):
        dr[name] = nc.dram_tensor(name, list(shape), dt, kind="ExternalInput").ap()
        return dr[name]

    x_own = din("x_own", [D, T])
    x_oth = din("x_oth", [D, T])
    p_own = din("p_own", [2, 256, T])
    p_oth = din("p_oth", [256, T])
    W = {}
    W["w_in"] = din("w_in", [NL, D, 11776])
    W["pool_w"] = din("pool_w", [NL, 4, 128, 128])
    W["w_br_pool"] = din("w_br_pool", [NL, 512, D])
    W["w_br_sb"] = din("w_br_sb", [NL, 1024, D])
    W["w_br_hg"] = din("w_br_hg", [NL, 512, D])
    W["w_out"] = din("w_out", [NL, D, D])
    W["w_gate_up"] = din("w_gate_up", [NL, D, 2 * DFF])
    W["w_down"] = din("w_down", [NL, DFF, D])
    W["w_ple_gate"] = din("w_ple_gate", [NL, D, D])
    W["w_ple_proj"] = din("w_ple_proj", [NL, 256, D])
    prm_d = din("prm", [128, 2, 64])
    normf_d = din("normf", [128, 16])
    flag_d = din("flag", [128, 1])
    cb_d = din("cb", [128, 7, 128])
    bdm_d = din("bdm", [128, 128])
    rmask_d = din("rmask", [128, T])
    invc_d = din("invc", [128, 4, 16])
    zK = din("zK", [1024, T], BF16)
    zV = din("zV", [T, 1024], BF16)
    zS = din("zS", [4, 128, 128], BF16)
    zH = din("zH", [512, 16])
    out_d = nc.dram_tensor("out", [D, T], F32, kind="ExternalOutput").ap()
    dbgt = {}
    if dbg:
        for nm, dt in (("dbg_h", BF16), ("dbg_yb", BF16), ("dbg_xmix", F32), ("dbg_xffn", F32), ("dbg_xple", F32)):
            dbgt[nm] = nc.dram_tensor(nm, [128, NKC, T], dt, kind="ExternalOutput").ap()

    class Exp_:
        pass

    def mk_exp(tag):
        e = Exp_()
        e.K = nc.dram_tensor("EK" + tag, [1024, T], BF16, kind="Internal").ap()
        e.V = nc.dram_tensor("EV" + tag, [T, 1024], BF16, kind="Internal").ap()
        e.S = nc.dram_tensor("ES" + tag, [4, 128, 128], BF16, kind="Internal").ap()
        e.H = nc.dram_tensor("EH" + tag, [512, 16], F32, kind="Internal").ap()
        e.rK = [Res("ek%s%d" % (tag, h)) for h in range(8)]
        e.rV = [Res("ev%s%d" % (tag, j)) for j in range(4)]
        e.rS = Res("es" + tag)
        e.rH = Res("eh" + tag)
        return e
    zexp = Exp_()
    zexp.K, zexp.V, zexp.S, zexp.H = zK, zV, zS, zH
    zexp.rK = [Res("zk")] * 8
    zexp.rV = [Res("zv")] * 4
    zexp.rS = Res("zs")
    zexp.rH = Res("zh")

    st = contextlib.ExitStack()
    with st:
        def sb(name, shape, dt):
            return st.enter_context(nc.sbuf_tensor("sb_" + name, list(shape), dt))
        xT = sb("xT", [128, NKC, T], F32)
        hT = sb("hT", [128, NKC, T], BF16)
        wp = [sb("wp%d" % i, [128, 4096], BF16) for i in range(3)]
        yb = sb("yb", [128, 16, T], BF16)
        mx = sb("mx", [128, 8, T], BF16)
        fs = [sb("fs%d" % i, [128, 1040], F32) for i in range(6)]
        cbt = sb("cbt", [128, 7, 128], BF16)
        bdm = sb("bdm", [128, 128], F32)
        rmask = sb("rmask", [128, T], F32)
        prm = sb("prm", [128, 2, 64], F32)
        normf = sb("normf", [128, 16], F32)
        flag = sb("flag", [128, 1], F32)
        invc = sb("invc", [128, 4, 16], F32)
        lbp = sb("lbp", [128, 16], F32)
        dec = sb("dec", [128, 2, 16], F32)
        rsn = sb("rsn", [128, 512], F32)
        sqr = sb("sqr", [128, 2, 512], BF16)
        epst = sb("epst", [128, 1], F32)
        fx = sb("fx", [128, 16], F32)
        Sst = sb("Sst", [128, 4, 2, 128], BF16)
        poolw = sb("poolw", [128, 4, 128], BF16)
        psA = st.enter_context(nc.psum_tensor("psA", [128, 1024], F32))
        psB = st.enter_context(nc.psum_tensor("psB", [128, 1024], F32))
        ps4_ = st.enter_context(nc.psum_tensor("pst4", [128, 512], F32))
        ps5_ = st.enter_context(nc.psum_tensor("pst5", [128, 512], F32))
        ps6_ = st.enter_context(nc.psum_tensor("pst6", [128, 512], F32))
        ps7_ = st.enter_context(nc.psum_tensor("pst7", [128, 512], F32))
        pair = [psA, psB]
        ps = [psA[:, 0:512], psA[:, 512:1024], psB[:, 0:512], psB[:, 512:1024], ps4_[:, :], ps5_[:, :], ps6_[:, :], ps7_[:, :]]
        psb = ps7_[:, :].bitcast(BF16)

        xres = [[Res("x%d_%d" % (dc, h)) for h in range(2)] for dc in range(NKC)]
        hres = [Res("hT0"), Res("hT1")]
        wres = [Res("wp0"), Res("wp1"), Res("wp2")]
        ybres = [Res("yb%d" % i) for i in range(16)]
        mxres = [Res("mx%d" % i) for i in range(8)]
        fsres = [Res("fs%d" % i) for i in range(6)]
        psres = [Res("ps%d" % i) for i in range(8)]
        psbres = psres[7]
        cres = Res("consts")
        lbres = Res("lbp")
        decres = [Res("dec0"), Res("dec1")]
        rsnres = Res("rsn")
        sqrres = [Res("sqr0"), Res("sqr1")]
        ssq_state = {"n": 0, "valid": False}
        fxres = Res("fx")
        Sres = [[Res("S%d_%d" % (h, i)) for i in range(2)] for h in range(4)]
        pwres = Res("poolw")
        state = {"w": 0, "ps": 0}

        TRI01, NEGTRI, NEGUI, IDENT, ONES, ZEROS, INV128 = 0, 1, 2, 3, 4, 5, 6

        def cb(i):
            return cbt[:, i, :]

        def getps():
            i = state["ps"]
            state["ps"] = (i + 1) % 5
            return Tl(ps[i], psres[i])

        def mm(out_ap, out_res, pairs, reads):
            pairs = list(pairs)

            def fn(e):
                n = len(pairs)
                ins = None
                for i, (l, r) in enumerate(pairs):
                    ins = e.matmul(out_ap, l, r, start=(i == 0), stop=(i == n - 1))
                return ins
            return P.op("pe", fn, reads=reads, writes=[out_res])

        def act(out_ap, in_ap, func, reads, writes, bias=None, scale=None):
            kw = {}
            if bias is not None:
                kw["bias"] = bias
            if scale is not None:
                kw["scale"] = scale
            return P.op("act", lambda e: e.activation(out=out_ap, in_=in_ap, func=func, **kw), reads=reads, writes=writes)

        def dve(fn, reads, writes):
            return P.op("dve", fn, reads=reads, writes=writes)

        def tt(out_ap, a, b, op, reads, writes):
            return dve(lambda e: e.tensor_tensor(out=out_ap, in0=a, in1=b, op=op), reads, writes)

        def ts(out_ap, a, s1, s2, op0, op1, reads, writes):
            if s2 is None:
                return dve(lambda e: e.tensor_scalar(out=out_ap, in0=a, scalar1=s1, scalar2=None, op0=op0), reads, writes)
            return dve(lambda e: e.tensor_scalar(out=out_ap, in0=a, scalar1=s1, scalar2=s2, op0=op0, op1=op1), reads, writes)

        def stt(out_ap, a, s, b, op0, op1, reads, writes):
            return dve(lambda e: e.scalar_tensor_tensor(out=out_ap, in0=a, scalar=s, in1=b, op0=op0, op1=op1), reads, writes)

        NSLOT = 3

        def wmulti(parts):
            i = state["w"]
            state["w"] = (i + 1) % NSLOT
            off = 0
            fns, views = [], []
            for (src2d, k0, nk, c0, n) in parts:
                view = wp[i][:, off:off + nk * n].rearrange("p (k c) -> p k c", c=n)
                src = src2d[k0 * 128:(k0 + nk) * 128, c0:c0 + n].rearrange("(k p) c -> p k c", p=128)
                fns.append(lambda e, view=view, src=src: e.dma_start(out=view, in_=src))
                views.append(view)
                off += nk * n
            assert off <= 4096
            P.dma("pool", "w%d" % i, fns, writes=[wres[i]])
            return views, wres[i]

        def wpanel(src2d, k0, nk, ranges, src_res=()):
            i = state["w"]
            state["w"] = (i + 1) % NSLOT
            ncols = sum(n for _, n in ranges)
            assert nk * ncols <= 4096
            view = wp[i][:, 0:nk * ncols].rearrange("p (k c) -> p k c", c=ncols)
            fns = []
            off = 0
            for c0, n in ranges:
                src = src2d[k0 * 128:(k0 + nk) * 128, c0:c0 + n].rearrange("(k p) c -> p k c", p=128)
                dst = view[:, :, off:off + n]
                fns.append(lambda e, dst=dst, src=src: e.dma_start(out=dst, in_=src))
                off += n
            P.dma("pool", "w%d" % i, fns, reads=list(src_res), writes=[wres[i]])
            return view, wres[i]

        hhalf = lambda kc, half: hT[:, kc, half * 512:(half + 1) * 512]

        def proj_h(panel, pres, col, half):
            b = getps()
            mm(b.ap[:, :], b.res, [(panel[:, kc, col:col + 128], hhalf(kc, half)) for kc in range(NKC)],
               reads=[pres, hres[half]])
            return b

        P.dma("pool", "c0", [lambda e: e.dma_start(out=cbt[:, :, :], in_=cb_d[:, :, :])], writes=[cres])
        P.dma("sp", "c1", [lambda e: e.dma_start(out=bdm[:, :], in_=bdm_d[:, :]),
                           lambda e: e.dma_start(out=rmask[:, :], in_=rmask_d[:, :]),
                           lambda e: e.dma_start(out=prm[:, :, :], in_=prm_d[:, :, :]),
                           lambda e: e.dma_start(out=normf[:, :], in_=normf_d[:, :]),
                           lambda e: e.dma_start(out=flag[:, :], in_=flag_d[:, :]),
                           lambda e: e.dma_start(out=invc[:, :, :], in_=invc_d[:, :, :])], writes=[cres])

        dve(lambda e: e.memset(epst[:, :], EPS), [], [cres])

        def load_x(src):
            for q in range(4):
                P.dma("sp", "x%d" % q,
                      [lambda e, q=q: e.dma_start(out=xT[:, 4 * q:4 * q + 4, :],
                                                  in_=src[512 * q:512 * (q + 1), :].rearrange("(c p) t -> p c t", p=128))],
                      writes=[xres[dc][h] for dc in range(4 * q, 4 * q + 4) for h in range(2)])

        def ssq_feed(dc, half):
            cs = slice(half * 512, (half + 1) * 512)
            i = ssq_state["n"] % 2
            ssq_state["n"] += 1
            s = Tl(sqr[:, i, :], sqrres[i])
            tt(s.ap, xT[:, dc, cs], xT[:, dc, cs], ALU.mult, [xres[dc][half]], [s.res])
            P.op("pe", lambda e, s=s, dc=dc, half=half: e.matmul(ps[5 + half][:, :], cb(ONES), s.ap,
                                                                 start=(dc == 0), stop=(dc == NKC - 1)),
                 reads=[s.res, cres], writes=[psres[5 + half]])
            ssq_state["valid"] = True

        def norm_stage(gcol_ap, to_out=None, pre=False):
            sq = [Tl(mx[:, 0, 0:512], mxres[0]), Tl(mx[:, 1, 0:512], mxres[1])]
            ssq_state["valid"] = False
            for half in range(2):
                cs = slice(half * 512, (half + 1) * 512)
                b = Tl(ps[5 + half], psres[5 + half]) if pre else getps()
                for dc in range(NKC if not pre else 0):
                    s = sq[dc % 2]
                    act(s.ap, xT[:, dc, cs], AF.Square, [xres[dc][half]], [s.res])
                    pr = [(cb(ONES), s.ap)]

                    def fn(e, dc=dc, s=s, b=b):
                        return e.matmul(b.ap[:, :], cb(ONES), s.ap, start=(dc == 0), stop=(dc == NKC - 1))
                    P.op("pe", fn, reads=[s.res, cres], writes=[b.res])
                rs = Tl(fs[5][:, cs], fsres[5])
                act(rs.ap, b.ap[:, :], AF.Ln, [b.res, cres], [rs.res], scale=1.0 / D, bias=epst[:, 0:1])
                act(rs.ap, rs.ap, AF.Exp, [rs.res], [rs.res], scale=-0.5)
                for dc in range(NKC):
                    if to_out is None:
                        stt(hT[:, dc, cs], xT[:, dc, cs], gcol_ap(dc), rs.ap, ALU.mult, ALU.mult,
                            [xres[dc][half], rs.res, cres], [hres[half]])
                    else:
                        o = Tl(fs[dc % 4][:, 0:512], fsres[dc % 4])
                        stt(o.ap, xT[:, dc, cs], gcol_ap(dc), rs.ap, ALU.mult, ALU.mult,
                            [xres[dc][half], rs.res, cres], [o.res])
                        P.dma("sp", "o%d" % (dc % 4),
                              [lambda e, o=o, dc=dc, cs=cs: e.dma_start(out=to_out[dc * 128:(dc + 1) * 128, cs], in_=o.ap)],
                              reads=[o.res])

        def kv_stage(L, ex):
            win = W["w_in"][L]
            for j in range(4):
                pan, pres = wpanel(win, 0, NKC, [(C_SK + j * 256, 256)])
                for sub in range(2):
                    h = 2 * j + sub
                    stg = Tl(mx[:, 6 + (h % 2), :], mxres[6 + (h % 2)])
                    for half in range(2):
                        b = proj_h(pan, pres, sub * 128, half)
                        act(stg.ap[:, half * 512:(half + 1) * 512], b.ap[:, :], AF.Copy, [b.res], [stg.res])
                    P.dma("sp", "ek%d" % (h % 2),
                          [lambda e, stg=stg, h=h: e.dma_start(out=ex.K[h * 128:(h + 1) * 128, :], in_=stg.ap)],
                          reads=[stg.res], writes=[ex.rK[h]])
            for j in range(4):
                pan, pres = wpanel(win, 0, NKC, [(C_SV + j * 256, 256)])
                base = 2 * (j % 2)
                stg_ap = mx[:, base:base + 2, :].rearrange("p a (t c) -> p (a t) c", c=256)
                stg_res = [mxres[base], mxres[base + 1]]
                for t8 in range(8):
                    b = getps()
                    mm(b.ap[:, 0:256], b.res,
                       [(hT[:, kc, t8 * 128:(t8 + 1) * 128], pan[:, kc, 0:256]) for kc in range(NKC)],
                       reads=[pres, hres[t8 // 4]])
                    act(stg_ap[:, t8, :], b.ap[:, 0:256], AF.Copy, [b.res], stg_res)
                P.dma("sp", "ev%d" % (j % 2),
                      [lambda e, stg_ap=stg_ap, j=j: e.dma_start(
                          out=ex.V[:, j * 256:(j + 1) * 256].rearrange("(t p) c -> p t c", p=128), in_=stg_ap)],
                      reads=stg_res, writes=[ex.rV[j]])

        def lb_setup(L):
            lb = lbp[:, 0:4]
            if L == 0:
                dve(lambda e: e.memset(lb, 0.0), [], [lbres])
            else:
                tt(lb, prm[:, 0, 56:60], prm[:, 1, 56:60], ALU.subtract, [cres], [lbres])
                act(lb, lb, AF.Exp, [lbres], [lbres])
                ts(lb, lb, 1.0, None, ALU.add, None, [lbres], [lbres])
                dve(lambda e: e.reciprocal(out=lb, in_=lb), [lbres], [lbres])
                ts(lb, lb, 0.0, 1.0, ALU.max, ALU.min, [lbres], [lbres])
            ts(lbp[:, 4:8], lb, 1e-20, None, ALU.max, None, [lbres], [lbres])
            ts(lbp[:, 8:12], lb, 1.0 - 1e-6, -1.0, ALU.min, ALU.mult, [lbres], [lbres])
            ts(lbp[:, 8:12], lbp[:, 8:12], 1.0, None, ALU.add, None, [lbres], [lbres])
            ts(lbp[:, 12:16], lb, -1.0, 1.0, ALU.mult, ALU.add, [lbres], [lbres])

        def hg_stage(L, ex, rem, pre):
            win = W["w_in"][L]
            lb_setup(L)
            Sall = Sst[:, :, 0, :]
            P.dma("sp", "srem", [lambda e: e.dma_start(out=Sall, in_=rem.S.rearrange("h k v -> k h v"))],
                  reads=[rem.rS], writes=[Sres[h][0] for h in range(4)])
            ts(Sall, Sall, flag[:, 0:1], None, ALU.mult, None, [cres] + [Sres[h][0] for h in range(4)],
               [Sres[h][0] for h in range(4)])
            E, Qf, R, Kf, Bt, BM = [Tl(fs[i][:, 0:T], fsres[i]) for i in range(6)]
            sets = [[Tl(mx[:, i, :], mxres[i]) for i in range(6)], [Tl(yb[:, i, :], ybres[i]) for i in range(6)]]
            SGs = [Tl(yb[:, 8, :], ybres[8]), Tl(yb[:, 9, :], ybres[9])]
            scm = [Tl(mx[:, 6, i * 128:(i + 1) * 128], Res("scm%d" % i)) for i in range(2)]
            alias([s.res for s in scm], [mxres[6]])
            osq = Tl(mx[:, 7, 0:512], mxres[7])
            O = [Tl(ps[5], psres[5]), Tl(ps[6], psres[6])]
            Bt3 = Bt.ap.rearrange("p (c j) -> p c j", j=64)
            BM3 = BM.ap.rearrange("p (c j) -> p c j", j=64)

            def projA(hd):
                if pre:
                    pan, pr = wpanel(win, 0, NKC, [(C_ZF + hd * 128, 128)])
                else:
                    pan, pr = wpanel(win, 0, NKC, [(C_ZF + hd * 128, 128), (C_HQ + hd * 128, 128)])
                panAs[hd] = (pan, pr)
                for half in range(2):
                    cs = slice(half * 512, (half + 1) * 512)
                    b = proj_h(pan, pr, 0, half)
                    act(E.ap[:, cs], b.ap[:, :], AF.Exp, [b.res], [E.res], scale=-1.0)
                    yield None

            def projB(hd):
                if pre:
                    return
                pan, pr = panAs[hd]
                for half in range(2):
                    cs = slice(half * 512, (half + 1) * 512)
                    b = proj_h(pan, pr, 128, half)
                    act(Qf.ap[:, cs], b.ap[:, :], AF.Silu, [b.res], [Qf.res])
                    yield None

            def elem(hd):
                par = hd % 2
                Q1, Q2, K2, K3, K3T, Vh = sets[par]
                SG = SGs[par]
                dect = dec[:, par, :]
                T1 = R
                if pre:
                    panB, prB = wpanel(win, 0, NKC, [(C_HV + hd * 128, 128)])
                    hvoff = 0
                else:
                    panB, prB = wpanel(win, 0, NKC, [(C_OG + hd * 128, 128), (C_HV + hd * 128, 128)])
                    hvoff = 128
                    for half in range(2):
                        cs = slice(half * 512, (half + 1) * 512)
                        b = proj_h(panB, prB, 0, half)
                        act(SG.ap[:, cs], b.ap[:, :], AF.Silu, [b.res], [SG.res])
                        yield None
                for g2 in range(2):
                    b = getps()
                    for t4 in range(4):
                        t8 = g2 * 4 + t4
                        mm(b.ap[:, t4 * 128:(t4 + 1) * 128], b.res,
                           [(hT[:, kc, t8 * 128:(t8 + 1) * 128], panB[:, kc, hvoff:hvoff + 128]) for kc in range(NKC)],
                           reads=[prB, hres[t8 // 4]])
                    act(Vh.ap[:, g2 * 512:(g2 + 1) * 512], b.ap[:, :], AF.Copy, [b.res], [Vh.res])
                    yield None
                act(R.ap, E.ap, AF.Ln, [E.res], [R.res], bias=1.0)
                yield None
                act(R.ap, R.ap, AF.Exp, [R.res], [R.res], scale=-1.0)
                yield None
                stt(Kf.ap, E.ap, lbp[:, 12 + hd:13 + hd], R.ap, ALU.mult, ALU.mult, [E.res, R.res, lbres], [Kf.res])
                yield "E_FREE"
                ts(R.ap, R.ap, lbp[:, 8 + hd:9 + hd], lbp[:, 4 + hd:5 + hd], ALU.mult, ALU.add, [R.res, lbres], [R.res])
                yield None
                act(R.ap, R.ap, AF.Ln, [R.res], [R.res])
                yield None
                dve(lambda e: e.tensor_tensor_scan(out=Bt.ap, data0=rmask[:, :], data1=R.ap, initial=0.0,
                                                   op0=ALU.mult, op1=ALU.add), [R.res, cres], [Bt.res])
                yield None
                if not pre:
                    tt(BM3, Bt3, Bt3[:, :, 31:32].to_broadcast([128, 16, 64]), ALU.subtract, [Bt.res], [BM.res])
                    yield None
                    act(T1.ap, BM.ap, AF.Exp, [BM.res], [T1.res])
                    yield None
                    tt(Q2.ap, Qf.ap, T1.ap, ALU.mult, [Qf.res, T1.res], [Q2.res])
                    yield None
                    act(T1.ap, BM.ap, AF.Exp, [BM.res], [T1.res], scale=-1.0)
                    yield None
                    tt(K2.ap, Kf.ap, T1.ap, ALU.mult, [Kf.res, T1.res], [K2.res])
                    yield None
                    act(T1.ap, Bt.ap, AF.Exp, [Bt.res], [T1.res])
                    yield None
                    tt(Q1.ap, Qf.ap, T1.ap, ALU.mult, [Qf.res, T1.res], [Q1.res])
                yield "Q_FREE"
                tt(BM3, Bt3, Bt3[:, :, 63:64].to_broadcast([128, 16, 64]), ALU.subtract, [Bt.res], [BM.res])
                yield None
                act(T1.ap, BM.ap, AF.Exp, [BM.res], [T1.res], scale=-1.0)
                yield None
                tt(K3.ap, Kf.ap, T1.ap, ALU.mult, [Kf.res, T1.res], [K3.res])
                yield None
                act(dect, Bt3[:, :, 63], AF.Exp, [Bt.res], [decres[par]])
                yield None
                for t8 in range(8):
                    P.op("pe", lambda e, t8=t8: e.transpose(psb[:, t8 * 128:(t8 + 1) * 128],
                                                            K3.ap[:, t8 * 128:(t8 + 1) * 128], cb(IDENT)),
                         reads=[K3.res, cres], writes=[psbres])
                act(K3T.ap, psb[:, :], AF.Copy, [psbres], [K3T.res])
                yield None

            def chunks(hd):
                par = hd % 2
                Q1, Q2, K2, K3, K3T, Vh = sets[par]
                SG = SGs[par]
                K3T3 = K3T.ap.rearrange("p (t c) -> p t c", c=128)
                Vh3 = Vh.ap.rearrange("p (t c) -> p t c", c=128)
                cur = 0
                for t8 in range(8):
                    tc_ = slice(t8 * 128, (t8 + 1) * 128)
                    Ob = O[t8 // 4]
                    oc0 = (t8 % 4) * 128
                    if not pre:
                        sc = getps()
                        mm(sc.ap[:, 0:128], sc.res, [(K2.ap[:, tc_], Q2.ap[:, tc_])], reads=[K2.res, Q2.res])
                        sm = scm[t8 % 2]
                        tt(sm.ap, sc.ap[:, 0:128], bdm[:, :], ALU.mult, [sc.res, cres], [sm.res])
                        P.op("pe", lambda e, Ob=Ob, oc0=oc0, t8=t8, sm=sm: e.matmul(
                            Ob.ap[:, oc0:oc0 + 128], Vh3[:, t8, :], sm.ap, start=True, stop=False),
                            reads=[Vh.res, sm.res], writes=[Ob.res])
                        yield
                    for cc in range(2):
                        c = 2 * t8 + cc
                        Sc = Sst[:, hd, cur, :]
                        if not pre:
                            P.op("pe", lambda e, Ob=Ob, oc0=oc0, cc=cc, c=c, Sc=Sc: e.matmul(
                                Ob.ap[:, oc0 + cc * 64:oc0 + cc * 64 + 64], Sc, Q1.ap[:, c * 64:(c + 1) * 64],
                                start=False, stop=True),
                                reads=[Sres[hd][cur], Q1.res], writes=[Ob.res])
                        su = getps()
                        p0 = cc * 64
                        mm(su.ap[:, 0:128], su.res, [(K3T3[p0:p0 + 64, t8, :], Vh3[p0:p0 + 64, t8, :])],
                           reads=[K3T.res, Vh.res])
                        Sn = Sst[:, hd, 1 - cur, :]
                        stt(Sn, Sc, dec[:, par, c:c + 1], su.ap[:, 0:128], ALU.mult, ALU.add,
                            [Sres[hd][cur], decres[par], su.res], [Sres[hd][1 - cur]])
                        cur = 1 - cur
                        yield
                assert cur == 0
                if pre:
                    return
                for half in range(2):
                    cs = slice(half * 512, (half + 1) * 512)
                    Ob = O[half]
                    act(osq.ap, Ob.ap[:, :], AF.Square, [Ob.res], [osq.res])
                    yield
                    b = getps()
                    mm(b.ap[:, :], b.res, [(cb(ONES), osq.ap)], reads=[osq.res, cres])
                    rs = Tl(rsn[:, :], rsnres)
                    act(rs.ap, b.ap[:, :], AF.Ln, [b.res, cres], [rs.res], scale=1.0 / 128.0, bias=epst[:, 0:1])
                    yield
                    act(rs.ap, rs.ap, AF.Exp, [rs.res], [rs.res], scale=-0.5)
                    yield
                    tt(rs.ap, Ob.ap[:, :], rs.ap, ALU.mult, [Ob.res, rs.res], [rs.res])
                    yield
                    stt(yb[:, 12 + hd, cs], SG.ap[:, cs], prm[:, L, 52 + hd:53 + hd], rs.ap, ALU.mult, ALU.mult,
                        [SG.res, rs.res, cres], [ybres[12 + hd]])
                    yield

            def run_step(C, El, PA, PB):
                e_free = El is None
                q_free = El is None
                live = {"C": C, "El": El, "PA": PA, "PB": PB}

                def adv(k):
                    g = live[k]
                    if g is None:
                        return None
                    try:
                        return next(g)
                    except StopIteration:
                        live[k] = None
                        return "DONE"
                while any(v is not None for v in live.values()):
                    adv("C")
                    tag = adv("El")
                    if tag == "E_FREE":
                        e_free = True
                    elif tag == "Q_FREE":
                        q_free = True
                    elif tag == "DONE":
                        e_free = q_free = True
                    if e_free:
                        adv("PA")
                    if q_free and live["PA"] is None:
                        adv("PB")

            panAs = {}
            run_step(None, None, projA(0), projB(0))
            run_step(None, elem(0), projA(1), projB(1))
            for hd in range(4):
                run_step(chunks(hd),
                         elem(hd + 1) if hd + 1 < 4 else None,
                         projA(hd + 2) if hd + 2 < 4 else None,
                         projB(hd + 2) if hd + 2 < 4 else None)
            unalias([mxres[6]], [s_.res for s_ in scm])
            P.dma("sp", "sexp", [lambda e: e.dma_start(out=ex.S.rearrange("h k v -> k h v"), in_=Sst[:, :, 0, :])],
                  reads=[Sres[h][0] for h in range(4)], writes=[ex.rS])

        def pool_stage(L, ex, rem, pre):
            win = W["w_in"][L]
            U = [Tl(fs[g], fsres[g]) for g in range(4)]
            tmp = [Tl(fs[4], fsres[4]), Tl(fs[5], fsres[5])]
            P.dma("sp", "hrem", [lambda e, g=g: e.dma_start(out=fs[g][:, 0:16], in_=rem.H[g * 128:(g + 1) * 128, :])
                                 for g in range(4)], reads=[rem.rH], writes=[u.res for u in U])
            for g in range(4):
                ts(U[g].ap[:, 0:16], U[g].ap[:, 0:16], flag[:, 0:1], None, ALU.mult, None, [U[g].res, cres], [U[g].res])
            if not pre:
                P.dma("pool", "pw", [lambda e: e.dma_start(out=poolw[:, :, :], in_=W["pool_w"][L].rearrange("g c d -> c g d"))],
                      writes=[pwres])
            for j in range(2):
                pan, pres = wpanel(win, 0, NKC, [(C_POOL + j * 256, 256)])
                for sub in range(2):
                    g = 2 * j + sub
                    for half in range(2):
                        b = proj_h(pan, pres, sub * 128, half)
                        act(U[g].ap[:, 16 + half * 512:16 + (half + 1) * 512], b.ap[:, :], AF.Copy, [b.res], [U[g].res])
            P.dma("sp", "hexp", [lambda e, g=g: e.dma_start(out=ex.H[g * 128:(g + 1) * 128, :], in_=fs[g][:, 1024:1040])
                                 for g in range(4)], reads=[u.res for u in U], writes=[ex.rH])
            if pre:
                return
            for g in range(4):
                w = 2 ** (g + 1)
                cur = U[g]
                for s in range(g + 1):
                    sh = 2 ** s
                    lo = 2 ** (s + 1) - 1
                    nxt = tmp[s % 2]
                    tt(nxt.ap[:, lo:1040], cur.ap[:, lo:1040], cur.ap[:, lo - sh:1040 - sh], ALU.add,
                       [cur.res], [nxt.res])
                    cur = nxt
                Wt = cur
                tt(fx[:, :], Wt.ap[:, 16:32], invc[:, g, :], ALU.mult, [Wt.res, cres], [fxres])
                ts(Wt.ap[:, 16:1040], Wt.ap[:, 16:1040], 1.0 / w, None, ALU.mult, None, [Wt.res], [Wt.res])
                dve(lambda e, Wt=Wt: e.tensor_copy(out=Wt.ap[:, 16:32], in_=fx[:, :]), [fxres, Wt.res], [Wt.res])
                Mb = Tl(mx[:, g % 2, :], mxres[g % 2])
                tt(Mb.ap, Wt.ap[:, 16:1040], U[g].ap[:, 16:1040], ALU.subtract, [Wt.res, U[g].res], [Mb.res])
                for half in range(2):
                    cs = slice(half * 512, (half + 1) * 512)
                    b = getps()
                    mm(b.ap[:, :], b.res, [(poolw[:, g, :], Mb.ap[:, cs])], reads=[pwres, Mb.res])
                    act(yb[:, g, cs], b.ap[:, :], AF.Copy, [b.res, cres], [ybres[g]], scale=prm[:, L, 48 + g:49 + g])

        def att_stage(L, ex, rem):
            win = W["w_in"][L]
            has_rem = rem is not zexp
            jlo = 0 if has_rem else 8
            kT = Tl(mx[:, 0:2, :].rearrange("p a t -> p (a t)"), None)
            kres = [mxres[0], mxres[1]]
            vt = mx[:, 2:4, :].rearrange("p a (j c) -> p (a j) c", c=128)
            vres = [mxres[2], mxres[3]]
            qT = Tl(mx[:, 4, :], mxres[4])
            spw = [Tl(mx[:, 5, :], mxres[5]), Tl(mx[:, 7, :], mxres[7])]
            Atw = Tl(mx[:, 6, :], mxres[6])
            etw = Tl(fs[0][:, 0:T], fsres[0])
            cf = [Tl(fs[2 + i][:, 0:T], fsres[2 + i]) for i in range(3)]
            cbsrc = [fs[1][:, 0:512].bitcast(BF16), fs[1][:, 512:1024].bitcast(BF16), fs[5][:, 0:512].bitcast(BF16)]
            cbres_ = [Res("cbf%d" % i) for i in range(3)]
            alias(cbres_[0:2], [fsres[1]])
            alias(cbres_[2:3], [fsres[5]])
            cbf = [Tl(cbsrc[i], cbres_[i]) for i in range(3)]
            O = [Tl(ps[5], psres[5]), Tl(ps[6], psres[6])]

            def pieces(jc):
                t0 = max(0, (jc - 8) * 128)
                out = []
                for half in range(2):
                    c0 = max(t0, half * 512)
                    c1 = (half + 1) * 512
                    if c0 < c1:
                        out.append((half, c0, c1))
                return out

            for h in range(8):
                pan, pres = wpanel(win, 0, NKC, [(C_SQ + h * 128, 128)])
                for half in range(2):
                    b = proj_h(pan, pres, 0, half)
                    act(qT.ap[:, half * 512:(half + 1) * 512], b.ap[:, :], AF.Copy, [b.res], [qT.res], scale=128.0 ** -0.5)
                kfn = [lambda e, h=h: e.dma_start(out=kT.ap[:, T:2 * T], in_=ex.K[h * 128:(h + 1) * 128, :])]
                vfn = [lambda e, h=h: e.dma_start(out=vt[:, 8:16, :], in_=ex.V[:, h * 128:(h + 1) * 128].rearrange("(j p) c -> p j c", p=128))]
                krd, vrd = [ex.rK[h]], [ex.rV[h // 2]]
                if has_rem:
                    kfn.append(lambda e, h=h: e.dma_start(out=kT.ap[:, 0:T], in_=rem.K[h * 128:(h + 1) * 128, :]))
                    vfn.append(lambda e, h=h: e.dma_start(out=vt[:, 0:8, :], in_=rem.V[:, h * 128:(h + 1) * 128].rearrange("(j p) c -> p j c", p=128)))
                    krd.append(rem.rK[h])
                    vrd.append(rem.rV[h // 2])
                P.dma("sp", "kl", kfn, reads=krd, writes=kres)
                P.dma("sp", "vl", vfn, reads=vrd, writes=vres)
                if has_rem:
                    ts(vt[:, 0:8, :], vt[:, 0:8, :], flag[:, 0:1], None, ALU.mult, None, vres + [cres], vres)
                for i in range(3):
                    dve(lambda e, i=i: e.memset(cf[i].ap, 0.0), [], [cf[i].res])
                    dve(lambda e, i=i: e.memset(cbf[i].ap, 0.0), [], [cbf[i].res])

                def t0_of(jc):
                    return max(0, (jc - 8) * 128)

                def halves_res(jc, par):
                    return [psres[2 * par + half] for (half, c0, c1) in pieces(jc)]

                def emit_Z(jc):
                    par = jc % 2
                    kc_ap = kT.ap[:, jc * 128:(jc + 1) * 128]
                    for (half, c0, c1) in pieces(jc):
                        mm(pair[par][:, c0:c1], psres[2 * par + half], [(kc_ap, qT.ap[:, c0:c1])], reads=kres + [qT.res])

                def emit_G(jc):
                    par = jc % 2
                    s_ = spw[par]
                    cbo = cbf[jc % 3]
                    for (half, c0, c1) in pieces(jc):
                        diag = (jc >= 8 and c0 == (jc - 8) * 128)

                        def fn(e, par=par, c0=c0, c1=c1, s_=s_, cbo=cbo, diag=diag):
                            if diag:
                                e.matmul(pair[par][:, c0:c0 + 128], cb(IDENT), cb(NEGTRI), start=False, stop=False)
                            e.matmul(pair[par][:, c0:c1], cb(NEGUI), s_.ap[:, c0:c1], start=False, stop=False)
                            return e.matmul(pair[par][:, c0:c1], cb(INV128), cbo.ap[:, c0:c1], start=False, stop=True)
                        P.op("pe", fn, reads=[s_.res, cbo.res, cres], writes=[psres[2 * par + half]])

                def emit_expE(jc):
                    par = jc % 2
                    t0 = t0_of(jc)
                    act(etw.ap[:, t0:T], pair[par][:, t0:T], AF.Exp, halves_res(jc, par), [etw.res])

                def emit_expA(jc):
                    par = jc % 2
                    t0 = t0_of(jc)
                    act(Atw.ap[:, t0:T], pair[par][:, t0:T], AF.Exp, halves_res(jc, par), [Atw.res])

                def emit_ln_tot(jc):
                    par = jc % 2
                    t0 = t0_of(jc)
                    s_ = spw[par]
                    act(s_.ap[:, t0:T], etw.ap[:, t0:T], AF.Ln, [etw.res], [s_.res], bias=1.0)
                    if jc >= 8:
                        tt(s_.ap[:, t0:t0 + 128], s_.ap[:, t0:t0 + 128], cb(TRI01), ALU.mult, [s_.res, cres], [s_.res])
                    if jc > jlo:
                        for (half, c0, c1) in pieces(jc):
                            wdt = c1 - c0
                            tot = Tl(ps[4] if half == 0 else ps[7], psres[4] if half == 0 else psres[7])
                            mm(tot.ap[:, 0:wdt], tot.res, [(cb(ONES), s_.ap[:, c0:c1])], reads=[s_.res, cres])
                            cn, co = cf[(jc - 1) % 3], cf[jc % 3]
                            tt(cn.ap[:, c0:c1], co.ap[:, c0:c1], tot.ap[:, 0:wdt], ALU.subtract,
                               [co.res, tot.res], [cn.res])
                            cbn = cbf[(jc - 1) % 3]
                            dve(lambda e, cbn=cbn, cn=cn, c0=c0, c1=c1: e.tensor_copy(out=cbn.ap[:, c0:c1], in_=cn.ap[:, c0:c1]),
                                [cn.res], [cbn.res])

                def emit_AV(jc):
                    for (half, c0, c1) in pieces(jc):
                        wdt = c1 - c0
                        last = (jc == jlo)
                        o0 = c0 - half * 512
                        P.op("pe", lambda e, half=half, o0=o0, wdt=wdt, c0=c0, c1=c1, jc=jc, last=last: e.matmul(
                            O[half].ap[:, o0:o0 + wdt], vt[:, jc, :], Atw.ap[:, c0:c1], start=False, stop=last),
                            reads=vres + [Atw.res], writes=[O[half].res])

                for half in range(2):
                    mm(O[half].ap[:, :], O[half].res, [(cb(ZEROS), qT.ap[:, half * 512:(half + 1) * 512])],
                       reads=[cres, qT.res])
                emit_Z(15)
                emit_expE(15)
                emit_ln_tot(15)
                for jc in range(15, jlo - 1, -1):
                    if jc > jlo:
                        emit_Z(jc - 1)
                    emit_G(jc)
                    if jc > jlo:
                        emit_expE(jc - 1)
                    emit_expA(jc)
                    emit_AV(jc)
                    if jc > jlo:
                        emit_ln_tot(jc - 1)
                for half in range(2):
                    act(yb[:, 4 + h, half * 512:(half + 1) * 512], O[half].ap[:, :], AF.Copy, [O[half].res], [ybres[4 + h]])
            unalias([fsres[1]], cbres_[0:2])
            unalias([fsres[5]], cbres_[2:3])

        def xadd(dc, half, b, reads_extra=(), feed=False):
            cs = slice(half * 512, (half + 1) * 512)
            tt(xT[:, dc, cs], xT[:, dc, cs], b.ap[:, :], ALU.add, [xres[dc][half], b.res] + list(reads_extra), [xres[dc][half]])
            if feed:
                ssq_feed(dc, half)

        def merge_stage(L):
            win = W["w_in"][L]
            brs = [(W["w_br_pool"][L], 4, 0), (W["w_br_sb"][L], 8, 4), (W["w_br_hg"][L], 4, 12)]
            acc = [[Tl(fs[sub][:, half * 512:(half + 1) * 512], Res("acc%d%d" % (sub, half))) for half in range(2)] for sub in range(2)]
            alias([acc[s][h].res for s in range(2) for h in range(2)], [fsres[0], fsres[1]])
            gs = [Tl(fs[2][:, i * 512:(i + 1) * 512], Res("gs%d" % i)) for i in range(2)]
            alias([g.res for g in gs], [fsres[2]])
            tm = [Tl(fs[3][:, i * 512:(i + 1) * 512], Res("tm%d" % i)) for i in range(2)]
            alias([g.res for g in tm], [fsres[3]])
            n = 0
            for grp in range(2):
                for dl in range(8):
                    dca = grp * 8 + dl
                    sub = dl % 2
                    for bi, (wbr, nkb, ybase) in enumerate(brs):
                        (gpan, bpan), pres_ = wmulti([(win, 0, NKC, C_GL + bi * D + dca * 128, 128),
                                                      (wbr, 0, nkb, dca * 128, 128)])
                        for half in range(2):
                            cs = slice(half * 512, (half + 1) * 512)
                            gb = proj_h(gpan, pres_, 0, half)
                            g_ = gs[n % 2]
                            t_ = tm[n % 2]
                            n += 1
                            act(g_.ap, gb.ap[:, :], AF.Sigmoid, [gb.res], [g_.res])
                            bb = getps()
                            mm(bb.ap[:, :], bb.res,
                               [(bpan[:, kc, :], yb[:, ybase + kc, cs]) for kc in range(nkb)],
                               reads=[pres_] + [ybres[ybase + kc] for kc in range(nkb)])
                            a_ = acc[sub][half]
                            if bi == 0:
                                tt(a_.ap, g_.ap, bb.ap[:, :], ALU.mult, [g_.res, bb.res], [a_.res])
                            else:
                                tt(t_.ap, g_.ap, bb.ap[:, :], ALU.mult, [g_.res, bb.res], [t_.res])
                                if bi == 1:
                                    tt(a_.ap, a_.ap, t_.ap, ALU.add, [a_.res, t_.res], [a_.res])
                                else:
                                    tt(mx[:, dl, cs], a_.ap, t_.ap, ALU.add, [a_.res, t_.res], [mxres[dl]])
                wo = W["w_out"][L]
                for op_ in range(8):
                    pan, pres = wpanel(wo, grp * 8, 8, [(op_ * 256, 256)])
                    for sub in range(2):
                        oc = op_ * 2 + sub
                        for half in range(2):
                            cs = slice(half * 512, (half + 1) * 512)
                            b = getps()
                            mm(b.ap[:, :], b.res, [(pan[:, kc, sub * 128:(sub + 1) * 128], mx[:, kc, cs]) for kc in range(8)],
                               reads=[pres] + mxres)
                            xadd(oc, half, b, feed=(grp == 1))
            unalias([fsres[0], fsres[1]], [acc[s_][h_].res for s_ in range(2) for h_ in range(2)])
            unalias([fsres[2]], [g_.res for g_ in gs])
            unalias([fsres[3]], [g_.res for g_ in tm])

        def ffn_stage(L):
            wgu = W["w_gate_up"][L]
            wd = W["w_down"][L]
            sl = [Tl(fs[i // 2][:, (i % 2) * 512:(i % 2 + 1) * 512], Res("sl%d" % i)) for i in range(4)]
            alias([s.res for s in sl], [fsres[0], fsres[1]])
            n = 0
            for fg in range(4):
                for fc in range(11):
                    c0 = fg * 1408 + fc * 128
                    (gp, up), pr_ = wmulti([(wgu, 0, NKC, c0, 128), (wgu, 0, NKC, DFF + c0, 128)])
                    for half in range(2):
                        cs = slice(half * 512, (half + 1) * 512)
                        gb = proj_h(gp, pr_, 0, half)
                        s_ = sl[n % 4]
                        n += 1
                        act(s_.ap, gb.ap[:, :], AF.Silu, [gb.res], [s_.res])
                        ub = proj_h(up, pr_, 0, half)
                        tt(yb[:, fc, cs], s_.ap, ub.ap[:, :], ALU.mult, [s_.res, ub.res], [ybres[fc]])
                for op_ in range(8):
                    pan, pres = wpanel(wd, fg * 11, 11, [(op_ * 256, 256)])
                    for sub in range(2):
                        oc = op_ * 2 + sub
                        for half in range(2):
                            cs = slice(half * 512, (half + 1) * 512)
                            b = getps()
                            mm(b.ap[:, :], b.res, [(pan[:, kc, sub * 128:(sub + 1) * 128], yb[:, kc, cs]) for kc in range(11)],
                               reads=[pres] + ybres[0:11])
                            xadd(oc, half, b, feed=(fg == 3))
            unalias([fsres[0], fsres[1]], [s_.res for s_ in sl])

        def ple_stage(L, p_src):
            pb = mx[:, 0:2, :]
            pbres = [mxres[0], mxres[1]]
            P.dma("pool", "pl", [lambda e: e.dma_start(out=pb, in_=p_src.rearrange("(k p) t -> p k t", p=128))], writes=pbres)
            sl = [Tl(fs[i // 2][:, (i % 2) * 512:(i % 2 + 1) * 512], Res("pl%d" % i)) for i in range(4)]
            alias([s.res for s in sl], [fsres[0], fsres[1]])
            n = 0
            for dc in range(16):
                (gp, pp), pr_ = wmulti([(W["w_ple_gate"][L], 0, NKC, dc * 128, 128), (W["w_ple_proj"][L], 0, 2, dc * 128, 128)])
                for half in range(2):
                    cs = slice(half * 512, (half + 1) * 512)
                    gb = proj_h(gp, pr_, 0, half)
                    s_ = sl[n % 4]
                    n += 1
                    act(s_.ap, gb.ap[:, :], AF.Sigmoid, [gb.res], [s_.res])
                    b = getps()
                    mm(b.ap[:, :], b.res, [(pp[:, kc, :], pb[:, kc, cs]) for kc in range(2)], reads=[pr_] + pbres)
                    tt(s_.ap, s_.ap, b.ap[:, :], ALU.mult, [s_.res, b.res], [s_.res])
                    tt(xT[:, dc, cs], xT[:, dc, cs], s_.ap, ALU.add, [xres[dc][half], s_.res], [xres[dc][half]])
                    ssq_feed(dc, half)
            unalias([fsres[0], fsres[1]], [s_.res for s_ in sl])

        def dump(nm, src_ap, res_list):
            if dbg and nm in dbgt:
                dst = dbgt.pop(nm)
                P.dma("sp", "dbg", [lambda e: e.dma_start(out=dst[:, :, :], in_=src_ap)], reads=res_list)

        allx = [xres[dc][h_] for dc in range(NKC) for h_ in range(2)]
        exps = {}
        for pi, ps_ in enumerate(passes):
            L = ps_["layer"]
            ex = mk_exp(str(pi))
            exps[ps_["name"]] = ex
            rem = zexp if ps_["remote"] is None else exps[ps_["remote"]]
            if ps_["x_src"] is not None:
                load_x({"own": x_own, "oth": x_oth}[ps_["x_src"]])
            norm_stage(lambda dc: prm[:, L, dc:dc + 1], pre=(ps_["x_src"] is None and ssq_state["valid"]))
            dump("dbg_h", hT[:, :, :], hres)
            kv_stage(L, ex)
            hg_stage(L, ex, rem, ps_["pre"])
            pool_stage(L, ex, rem, ps_["pre"])
            if ps_["pre"]:
                continue
            att_stage(L, ex, rem)
            dump("dbg_yb", yb[:, :, :], ybres)
            merge_stage(L)
            dump("dbg_xmix", xT[:, :, :], allx)
            norm_stage(lambda dc: prm[:, L, 16 + dc:17 + dc], pre=ssq_state["valid"])
            ffn_stage(L)
            dump("dbg_xffn", xT[:, :, :], allx)
            norm_stage(lambda dc: prm[:, L, 32 + dc:33 + dc], pre=ssq_state["valid"])
            ple_stage(L, {"own": p_own[L], "oth": p_oth}[ps_["p_src"]])
            dump("dbg_xple", xT[:, :, :], allx)
            if ps_.get("final"):
                norm_stage(lambda dc: normf[:, dc:dc + 1], to_out=out_d, pre=ssq_state["valid"])
        if "dbg" in P.dsem_cnt:
            P.wait_tok("sp", ("dbg", P.dsem_cnt["dbg"]))
        for q in range(4):
            if ("o%d" % q) in P.dsem_cnt:
                P.wait_tok("sp", ("o%d" % q, P.dsem_cnt["o%d" % q]))
        P.emit()
    return nc


PASSES_FUSED = [
    dict(name="p1", layer=0, x_src="oth", remote=None, pre=False, p_src="oth"),
    dict(name="p3", layer=1, x_src=None, remote=None, pre=True, p_src=None),
    dict(name="p2", layer=0, x_src="own", remote="p1", pre=False, p_src="own"),
    dict(name="p4", layer=1, x_src=None, remote="p3", pre=False, p_src="own", final=True),
]


def _consts():
    s = np.arange(128)[:, None]
    t = np.arange(128)[None, :]
    cb = np.zeros((128, 7, 128), np.float32)
    cb[:, 0, :] = (s < t)
    cb[:, 1, :] = np.where(s < t, 0.0, NEG)
    cb[:, 2, :] = np.where(s >= t, -1.0, 0.0)
    cb[:, 3, :] = (s == t)
    cb[:, 4, :] = 1.0
    cb[:, 6, :] = 1.0 / 128.0
    bdm = ((s // 64 == t // 64) & (s <= t)).astype(np.float32)
    rmask = np.ones((128, T), np.float32)
    rmask[:, ::64] = 0.0
    return cb, bdm, rmask


def _in_maps(inputs, NL=2):
    f = lambda a: np.ascontiguousarray(np.asarray(a, dtype=np.float32))
    x = f(inputs["x"])
    p = f(inputs["p"])
    cb, bdm, rmask = _consts()
    col = lambda v: np.ascontiguousarray(v.reshape(-1, 128).T)
    prm = np.zeros((128, 2, 64), np.float32)
    for L in range(2):
        prm[:, L, 0:16] = col(f(inputs["norm_mix"])[L])
        prm[:, L, 16:32] = col(f(inputs["norm_ffn"])[L])
        prm[:, L, 32:48] = col(f(inputs["norm_ple"])[L])
        prm[:, L, 48:52] = col(f(inputs["pool_scale"])[L])
        prm[:, L, 52:56] = col(f(inputs["hg_norm"])[L])
        prm[:, L, 56:60] = col(f(inputs["hg_lb"])[L])
    normf = col(f(inputs["norm_final"]))
    ws = {k: f(inputs[k])[0:NL] for k in WNAMES}
    zK = np.zeros((1024, T), ml_dtypes.bfloat16)
    zV = np.zeros((T, 1024), ml_dtypes.bfloat16)
    zS = np.zeros((4, 128, 128), ml_dtypes.bfloat16)
    zH = np.zeros((512, 16), np.float32)
    maps = []
    for c in range(8):
        b, role = c // 2, c % 2
        m = dict(ws)
        own = x[b, role * T:(role + 1) * T, :]
        m["x_own"] = np.ascontiguousarray(own.T)
        m["p_own"] = np.ascontiguousarray(p[:, b, role * T:(role + 1) * T, :].transpose(0, 2, 1))
        if role == 1:
            m["x_oth"] = np.ascontiguousarray(x[b, 0:T, :].T)
            m["p_oth"] = np.ascontiguousarray(p[0, b, 0:T, :].T)
        else:
            m["x_oth"] = np.zeros((D, T), np.float32)
            m["p_oth"] = np.zeros((256, T), np.float32)
        m["prm"] = prm
        m["normf"] = normf
        m["flag"] = np.full((128, 1), float(role), np.float32)
        m["cb"] = cb
        m["bdm"] = bdm
        m["rmask"] = rmask
        invc = np.zeros((128, 4, 16), np.float32)
        for g in range(4):
            w = 2 ** (g + 1)
            pos = np.arange(16) + role * T
            invc[:, g, :] = 1.0 / np.minimum(pos + 1, w)
        m["invc"] = invc
        m["zK"], m["zV"], m["zS"], m["zH"] = zK, zV, zS, zH
        maps.append(m)
    return maps


_NC_CACHE = {}


def kernel(**inputs):
    if "fused" not in _NC_CACHE:
        _NC_CACHE["fused"] = build(PASSES_FUSED)
    nc = _NC_CACHE["fused"]
    maps = _in_maps(inputs)
    res = run_bass_kernel_spmd(nc, maps, core_ids=list(range(8)))
    out = np.zeros((4, 2 * T, D), np.float32)
    for c in range(8):
        b, role = c // 2, c % 2
        out[b, role * T:(role + 1) * T, :] = np.asarray(res.results[c]["out"], dtype=np.float32).T
    return out
```

```python
import contextlib
import numpy as np
import ml_dtypes
import concourse.bass as bass
import concourse.mybir as mybir
from concourse.bass_utils import run_bass_kernel_spmd

F32 = mybir.dt.float32
BF16 = mybir.dt.bfloat16
ALU = mybir.AluOpType
AF = mybir.ActivationFunctionType

D = 2048
T = 1024
NKC = 16
DFF = 5632
EPS = 1e-6
C_POOL, C_SQ, C_SK, C_SV, C_ZF, C_HV, C_HQ, C_OG, C_GL = 0, 512, 1536, 2560, 3584, 4096, 4608, 5120, 5632
NEG = -30000.0
WNAMES = ["w_in", "pool_w", "w_br_pool", "w_br_sb", "w_br_hg", "w_out", "w_gate_up", "w_down",
          "w_ple_gate", "w_ple_proj"]


class Res:
    __slots__ = ("name", "w", "r")

    def __init__(self, name):
        self.name = name
        self.w = None
        self.r = []


def alias(new, old):
    toks = []
    for o in old:
        if o.w is not None:
            toks.append(o.w)
        toks.extend(o.r)
    for n in new:
        n.w = None
        n.r = list(toks)


def unalias(parents, subs):
    toks = []
    for o in subs:
        if o.w is not None:
            toks.append(o.w)
        toks.extend(o.r)
    for p in parents:
        p.r = list(p.r) + toks


class Prog:
    ENGS = ("pe", "act", "dve", "pool", "sp")

    def __init__(self, nc):
        self.nc = nc
        self.ops = {e: [] for e in self.ENGS}
        self.cnt = {e: 0 for e in self.ENGS}
        self.seen = {e: {} for e in self.ENGS}
        self.dsem_cnt = {}
        self.sem_names = set(self.ENGS)

    def _deps(self, eng, reads, writes):
        need = {}

        def add(tok):
            if tok is None:
                return
            s, v = tok
            if s == "pe" and eng == "pe":
                return
            if need.get(s, 0) < v:
                need[s] = v
        for r in reads:
            add(r.w)
        for w in writes:
            add(w.w)
            for t in w.r:
                add(t)
        waits = []
        seen = self.seen[eng]
        for s, v in need.items():
            if seen.get(s, 0) < v:
                seen[s] = v
                waits.append((s, v))
        return waits

    def _commit(self, tok, reads, writes):
        for r in reads:
            if len(r.r) > 64:
                mx = {}
                for s, v in r.r:
                    if mx.get(s, 0) < v:
                        mx[s] = v
                r.r = list(mx.items())
            r.r.append(tok)
        for w in writes:
            w.w = tok
            w.r = []

    def op(self, eng, fn, reads=(), writes=()):
        waits = self._deps(eng, reads, writes)
        self.cnt[eng] += 1
        tok = (eng, self.cnt[eng])
        self.ops[eng].append((waits, fn, (eng, 1)))
        self._commit(tok, reads, writes)
        return tok

    def dma(self, queue, sem, fns, reads=(), writes=()):
        self.sem_names.add(sem)
        prev = self.dsem_cnt.get(sem, 0)
        waits = self._deps(queue, reads, writes)
        if prev and self.seen[queue].get(sem, 0) < prev:
            self.seen[queue][sem] = prev
            waits.append((sem, prev))
        first = True
        for fn in fns:
            self.ops[queue].append((waits if first else [], fn, (sem, 16)))
            first = False
        tot = prev + 16 * len(fns)
        self.dsem_cnt[sem] = tot
        tok = (sem, tot)
        self._commit(tok, reads, writes)
        return tok

    def wait_tok(self, eng, tok):
        s, v = tok
        if self.seen[eng].get(s, 0) < v:
            self.seen[eng][s] = v
            self.ops[eng].append(([(s, v)], None, None))

    def emit(self):
        nc = self.nc
        with contextlib.ExitStack() as st:
            sems = {}
            for name in sorted(self.sem_names):
                sems[name] = st.enter_context(nc.semaphore("s_" + name))
            block = st.enter_context(nc.Block())

            def run(engname):
                def body(e):
                    for waits, fn, inc in self.ops[engname]:
                        for s, v in waits:
                            e.wait_ge(sems[s], v)
                        if fn is not None:
                            ins = fn(e)
                            if inc is not None:
                                ins.then_inc(sems[inc[0]], inc[1])
                return body
            block.tensor(run("pe"))
            block.scalar(run("act"))
            block.vector(run("dve"))
            block.gpsimd(run("pool"))
            block.sync(run("sp"))


class Tl:
    __slots__ = ("ap", "res")

    def __init__(self, ap, res):
        self.ap = ap
        self.res = res


def build(passes, final=True, NL=2, dbg=False):
    nc = bass.Bass("TRN2", target_bir_lowering=False)
    P = Prog(nc)
    dr = {}

    def din(name, shape, dt=F32):
        dr[name] = nc.dram_tensor(name, list(shape), dt, kind="ExternalInput").ap()
        return dr[name]

    x_own = din("x_own", [D, T])
    x_oth = din("x_oth", [D, T])
    p_own = din("p_own", [2, 256, T])
    p_oth = din("p_oth", [256, T])
    W = {}
    W["w_in"] = din("w_in", [NL, D, 11776])
    W["pool_w"] = din("pool_w", [NL, 4, 128, 128])
    W["w_br_pool"] = din("w_br_pool", [NL, 512, D])
    W["w_br_sb"] = din("w_br_sb", [NL, 1024, D])
    W["w_br_hg"] = din("w_br_hg", [NL, 512, D])
    W["w_out"] = din("w_out", [NL, D, D])
    W["w_gate_up"] = din("w_gate_up", [NL, D, 2 * DFF])
    W["w_down"] = din("w_down", [NL, DFF, D])
    W["w_ple_gate"] = din("w_ple_gate", [NL, D, D])
    W["w_ple_proj"] = din("w_ple_proj", [NL, 256, D])
    prm_d = din("prm", [128, 2, 64])
    normf_d = din("normf", [128, 16])
    flag_d = din("flag", [128, 1])
    cb_d = din("cb", [128, 7, 128])
    bdm_d = din("bdm", [128, 128])
    rmask_d = din("rmask", [128, T])
    invc_d = din("invc", [128, 4, 16])
    zK = din("zK", [1024, T], BF16)
    zV = din("zV", [T, 1024], BF16)
    zS = din("zS", [4, 128, 128], BF16)
    zH = din("zH", [512, 16])
    out_d = nc.dram_tensor("out", [D, T], F32, kind="ExternalOutput").ap()
    dbgt = {}
    if dbg:
        for nm, dt in (("dbg_h", BF16), ("dbg_yb", BF16), ("dbg_xmix", F32), ("dbg_xffn", F32), ("dbg_xple", F32)):
            dbgt[nm] = nc.dram_tensor(nm, [128, NKC, T], dt, kind="ExternalOutput").ap()

    class Exp_:
        pass

    def mk_exp(tag):
        e = Exp_()
        e.K = nc.dram_tensor("EK" + tag, [1024, T], BF16, kind="Internal").ap()
        e.V = nc.dram_tensor("EV" + tag, [T, 1024], BF16, kind="Internal").ap()
        e.S = nc.dram_tensor("ES" + tag, [4, 128, 128], BF16, kind="Internal").ap()
        e.H = nc.dram_tensor("EH" + tag, [512, 16], F32, kind="Internal").ap()
        e.rK = [Res("ek%s%d" % (tag, h)) for h in range(8)]
        e.rV = [Res("ev%s%d" % (tag, j)) for j in range(4)]
        e.rS = Res("es" + tag)
        e.rH = Res("eh" + tag)
        return e
    zexp = Exp_()
    zexp.K, zexp.V, zexp.S, zexp.H = zK, zV, zS, zH
    zexp.rK = [Res("zk")] * 8
    zexp.rV = [Res("zv")] * 4
    zexp.rS = Res("zs")
    zexp.rH = Res("zh")

    st = contextlib.ExitStack()
    with st:
        def sb(name, shape, dt):
            return st.enter_context(nc.sbuf_tensor("sb_" + name, list(shape), dt))
        xT = sb("xT", [128, NKC, T], F32)
        hT = sb("hT", [128, NKC, T], BF16)
        wp = [sb("wp%d" % i, [128, 4096], BF16) for i in range(3)]
        yb = sb("yb", [128, 16, T], BF16)
        mx = sb("mx", [128, 8, T], BF16)
        fs = [sb("fs%d" % i, [128, 1040], F32) for i in range(6)]
        cbt = sb("cbt", [128, 7, 128], BF16)
        bdm = sb("bdm", [128, 128], F32)
        rmask = sb("rmask", [128, T], F32)
        prm = sb("prm", [128, 2, 64], F32)
        normf = sb("normf", [128, 16], F32)
        flag = sb("flag", [128, 1], F32)
        invc = sb("invc", [128, 4, 16], F32)
        lbp = sb("lbp", [128, 16], F32)
        dec = sb("dec", [128, 2, 16], F32)
        rsn = sb("rsn", [128, 512], F32)
        sqr = sb("sqr", [128, 2, 512], BF16)
        epst = sb("epst", [128, 1], F32)
        fx = sb("fx", [128, 16], F32)
        Sst = sb("Sst", [128, 4, 2, 128], BF16)
        poolw = sb("poolw", [128, 4, 128], BF16)
        psA = st.enter_context(nc.psum_tensor("psA", [128, 1024], F32))
        psB = st.enter_context(nc.psum_tensor("psB", [128, 1024], F32))
        ps4_ = st.enter_context(nc.psum_tensor("pst4", [128, 512], F32))
        ps5_ = st.enter_context(nc.psum_tensor("pst5", [128, 512], F32))
        ps6_ = st.enter_context(nc.psum_tensor("pst6", [128, 512], F32))
        ps7_ = st.enter_context(nc.psum_tensor("pst7", [128, 512], F32))
        pair = [psA, psB]
        ps = [psA[:, 0:512], psA[:, 512:1024], psB[:, 0:512], psB[:, 512:1024], ps4_[:, :], ps5_[:, :], ps6_[:, :], ps7_[:, :]]
        psb = ps7_[:, :].bitcast(BF16)

        xres = [[Res("x%d_%d" % (dc, h)) for h in range(2)] for dc in range(NKC)]
        hres = [Res("hT0"), Res("hT1")]
        wres = [Res("wp0"), Res("wp1"), Res("wp2")]
        ybres = [Res("yb%d" % i) for i in range(16)]
        mxres = [Res("mx%d" % i) for i in range(8)]
        fsres = [Res("fs%d" % i) for i in range(6)]
        psres = [Res("ps%d" % i) for i in range(8)]
        psbres = psres[7]
        cres = Res("consts")
        lbres = Res("lbp")
        decres = [Res("dec0"), Res("dec1")]
        rsnres = Res("rsn")
        sqrres = [Res("sqr0"), Res("sqr1")]
        ssq_state = {"n": 0, "valid": False}
        fxres = Res("fx")
        Sres = [[Res("S%d_%d" % (h, i)) for i in range(2)] for h in range(4)]
        pwres = Res("poolw")
        state = {"w": 0, "ps": 0}

        TRI01, NEGTRI, NEGUI, IDENT, ONES, ZEROS, INV128 = 0, 1, 2, 3, 4, 5, 6

        def cb(i):
            return cbt[:, i, :]

        def getps():
            i = state["ps"]
            state["ps"] = (i + 1) % 5
            return Tl(ps[i], psres[i])

        def mm(out_ap, out_res, pairs, reads):
            pairs = list(pairs)

            def fn(e):
                n = len(pairs)
                ins = None
                for i, (l, r) in enumerate(pairs):
                    ins = e.matmul(out_ap, l, r, start=(i == 0), stop=(i == n - 1))
                return ins
            return P.op("pe", fn, reads=reads, writes=[out_res])

        def act(out_ap, in_ap, func, reads, writes, bias=None, scale=None):
            kw = {}
            if bias is not None:
                kw["bias"] = bias
            if scale is not None:
                kw["scale"] = scale
            return P.op("act", lambda e: e.activation(out=out_ap, in_=in_ap, func=func, **kw), reads=reads, writes=writes)

        def dve(fn, reads, writes):
            return P.op("dve", fn, reads=reads, writes=writes)

        def tt(out_ap, a, b, op, reads, writes):
            return dve(lambda e: e.tensor_tensor(out=out_ap, in0=a, in1=b, op=op), reads, writes)

        def ts(out_ap, a, s1, s2, op0, op1, reads, writes):
            if s2 is None:
                return dve(lambda e: e.tensor_scalar(out=out_ap, in0=a, scalar1=s1, scalar2=None, op0=op0), reads, writes)
            return dve(lambda e: e.tensor_scalar(out=out_ap, in0=a, scalar1=s1, scalar2=s2, op0=op0, op1=op1), reads, writes)

        def stt(out_ap, a, s, b, op0, op1, reads, writes):
            return dve(lambda e: e.scalar_tensor_tensor(out=out_ap, in0=a, scalar=s, in1=b, op0=op0, op1=op1), reads, writes)

        NSLOT = 3

        def wmulti(parts):
            i = state["w"]
            state["w"] = (i + 1) % NSLOT
            off = 0
            fns, views = [], []
            for (src2d, k0, nk, c0, n) in parts:
                view = wp[i][:, off:off + nk * n].rearrange("p (k c) -> p k c", c=n)
                src = src2d[k0 * 128:(k0 + nk) * 128, c0:c0 + n].rearrange("(k p) c -> p k c", p=128)
                fns.append(lambda e, view=view, src=src: e.dma_start(out=view, in_=src))
                views.append(view)
                off += nk * n
            assert off <= 4096
            P.dma("pool", "w%d" % i, fns, writes=[wres[i]])
            return views, wres[i]

        def wpanel(src2d, k0, nk, ranges, src_res=()):
            i = state["w"]
            state["w"] = (i + 1) % NSLOT
            ncols = sum(n for _, n in ranges)
            assert nk * ncols <= 4096
            view = wp[i][:, 0:nk * ncols].rearrange("p (k c) -> p k c", c=ncols)
            fns = []
            off = 0
            for c0, n in ranges:
                src = src2d[k0 * 128:(k0 + nk) * 128, c0:c0 + n].rearrange("(k p) c -> p k c", p=128)
                dst = view[:, :, off:off + n]
                fns.append(lambda e, dst=dst, src=src: e.dma_start(out=dst, in_=src))
                off += n
            P.dma("pool", "w%d" % i, fns, reads=list(src_res), writes=[wres[i]])
            return view, wres[i]

        hhalf = lambda kc, half: hT[:, kc, half * 512:(half + 1) * 512]

        def proj_h(panel, pres, col, half):
            b = getps()
            mm(b.ap[:, :], b.res, [(panel[:, kc, col:col + 128], hhalf(kc, half)) for kc in range(NKC)],
               reads=[pres, hres[half]])
            return b

        P.dma("pool", "c0", [lambda e: e.dma_start(out=cbt[:, :, :], in_=cb_d[:, :, :])], writes=[cres])
        P.dma("sp", "c1", [lambda e: e.dma_start(out=bdm[:, :], in_=bdm_d[:, :]),
                           lambda e: e.dma_start(out=rmask[:, :], in_=rmask_d[:, :]),
                           lambda e: e.dma_start(out=prm[:, :, :], in_=prm_d[:, :, :]),
                           lambda e: e.dma_start(out=normf[:, :], in_=normf_d[:, :]),
                           lambda e: e.dma_start(out=flag[:, :], in_=flag_d[:, :]),
                           lambda e: e.dma_start(out=invc[:, :, :], in_=invc_d[:, :, :])], writes=[cres])

        dve(lambda e: e.memset(epst[:, :], EPS), [], [cres])

        def load_x(src):
            for q in range(4):
                P.dma("sp", "x%d" % q,
                      [lambda e, q=q: e.dma_start(out=xT[:, 4 * q:4 * q + 4, :],
                                                  in_=src[512 * q:512 * (q + 1), :].rearrange("(c p) t -> p c t", p=128))],
                      writes=[xres[dc][h] for dc in range(4 * q, 4 * q + 4) for h in range(2)])

        def ssq_feed(dc, half):
            cs = slice(half * 512, (half + 1) * 512)
            i = ssq_state["n"] % 2
            ssq_state["n"] += 1
            s = Tl(sqr[:, i, :], sqrres[i])
            ssq_flush(0)
            tt(s.ap, xT[:, dc, cs], xT[:, dc, cs], ALU.mult, [xres[dc][half]], [s.res])

            def pe_part(s=s, dc=dc, half=half):
                P.op("pe", lambda e: e.matmul(ps[5 + half][:, :], cb(ONES), s.ap,
                                              start=(dc == 0), stop=(dc == NKC - 1)),
                     reads=[s.res, cres], writes=[psres[5 + half]])
            ssq_state.setdefault("pend", []).append(pe_part)
            ssq_state["valid"] = True

        def ssq_flush(keep):
            pend = ssq_state.setdefault("pend", [])
            while len(pend) > keep:
                pend.pop(0)()

        def norm_stage(gcol_ap, to_out=None, pre=False):
            sq = [Tl(mx[:, 0, 0:512], mxres[0]), Tl(mx[:, 1, 0:512], mxres[1])]
            ssq_flush(0)
            ssq_state["valid"] = False
            for half in range(2):
                cs = slice(half * 512, (half + 1) * 512)
                b = Tl(ps[5 + half], psres[5 + half]) if pre else getps()
                for dc in range(NKC if not pre else 0):
                    s = sq[dc % 2]
                    act(s.ap, xT[:, dc, cs], AF.Square, [xres[dc][half]], [s.res])
                    pr = [(cb(ONES), s.ap)]

                    def fn(e, dc=dc, s=s, b=b):
                        return e.matmul(b.ap[:, :], cb(ONES), s.ap, start=(dc == 0), stop=(dc == NKC - 1))
                    P.op("pe", fn, reads=[s.res, cres], writes=[b.res])
                rs = Tl(fs[5][:, cs], fsres[5])
                act(rs.ap, b.ap[:, :], AF.Ln, [b.res, cres], [rs.res], scale=1.0 / D, bias=epst[:, 0:1])
                act(rs.ap, rs.ap, AF.Exp, [rs.res], [rs.res], scale=-0.5)
                for dc in range(NKC):
                    if to_out is None:
                        stt(hT[:, dc, cs], xT[:, dc, cs], gcol_ap(dc), rs.ap, ALU.mult, ALU.mult,
                            [xres[dc][half], rs.res, cres], [hres[half]])
                    else:
                        o = Tl(fs[dc % 4][:, 0:512], fsres[dc % 4])
                        stt(o.ap, xT[:, dc, cs], gcol_ap(dc), rs.ap, ALU.mult, ALU.mult,
                            [xres[dc][half], rs.res, cres], [o.res])
                        P.dma("sp", "o%d" % (dc % 4),
                              [lambda e, o=o, dc=dc, cs=cs: e.dma_start(out=to_out[dc * 128:(dc + 1) * 128, cs], in_=o.ap)],
                              reads=[o.res])

        def kv_stage(L, ex):
            win = W["w_in"][L]
            for j in range(4):
                pan, pres = wpanel(win, 0, NKC, [(C_SK + j * 256, 256)])
                for sub in range(2):
                    h = 2 * j + sub
                    stg = Tl(mx[:, 6 + (h % 2), :], mxres[6 + (h % 2)])
                    for half in range(2):
                        b = proj_h(pan, pres, sub * 128, half)
                        act(stg.ap[:, half * 512:(half + 1) * 512], b.ap[:, :], AF.Copy, [b.res], [stg.res])
                    P.dma("sp", "ek%d" % (h % 2),
                          [lambda e, stg=stg, h=h: e.dma_start(out=ex.K[h * 128:(h + 1) * 128, :], in_=stg.ap)],
                          reads=[stg.res], writes=[ex.rK[h]])
            for j in range(4):
                pan, pres = wpanel(win, 0, NKC, [(C_SV + j * 256, 256)])
                base = 2 * (j % 2)
                stg_ap = mx[:, base:base + 2, :].rearrange("p a (t c) -> p (a t) c", c=256)
                stg_res = [mxres[base], mxres[base + 1]]
                for t8 in range(8):
                    b = getps()
                    mm(b.ap[:, 0:256], b.res,
                       [(hT[:, kc, t8 * 128:(t8 + 1) * 128], pan[:, kc, 0:256]) for kc in range(NKC)],
                       reads=[pres, hres[t8 // 4]])
                    act(stg_ap[:, t8, :], b.ap[:, 0:256], AF.Copy, [b.res], stg_res)
                P.dma("sp", "ev%d" % (j % 2),
                      [lambda e, stg_ap=stg_ap, j=j: e.dma_start(
                          out=ex.V[:, j * 256:(j + 1) * 256].rearrange("(t p) c -> p t c", p=128), in_=stg_ap)],
                      reads=stg_res, writes=[ex.rV[j]])

        def lb_setup(L):
            lb = lbp[:, 0:4]
            if L == 0:
                dve(lambda e: e.memset(lb, 0.0), [], [lbres])
            else:
                tt(lb, prm[:, 0, 56:60], prm[:, 1, 56:60], ALU.subtract, [cres], [lbres])
                act(lb, lb, AF.Exp, [lbres], [lbres])
                ts(lb, lb, 1.0, None, ALU.add, None, [lbres], [lbres])
                dve(lambda e: e.reciprocal(out=lb, in_=lb), [lbres], [lbres])
                ts(lb, lb, 0.0, 1.0, ALU.max, ALU.min, [lbres], [lbres])
            ts(lbp[:, 4:8], lb, 1e-20, None, ALU.max, None, [lbres], [lbres])
            ts(lbp[:, 8:12], lb, 1.0 - 1e-6, -1.0, ALU.min, ALU.mult, [lbres], [lbres])
            ts(lbp[:, 8:12], lbp[:, 8:12], 1.0, None, ALU.add, None, [lbres], [lbres])
            ts(lbp[:, 12:16], lb, -1.0, 1.0, ALU.mult, ALU.add, [lbres], [lbres])

        def hg_stage(L, ex, rem, pre):
            win = W["w_in"][L]
            lb_setup(L)
            Sall = Sst[:, :, 0, :]
            P.dma("sp", "srem", [lambda e: e.dma_start(out=Sall, in_=rem.S.rearrange("h k v -> k h v"))],
                  reads=[rem.rS], writes=[Sres[h][0] for h in range(4)])
            ts(Sall, Sall, flag[:, 0:1], None, ALU.mult, None, [cres] + [Sres[h][0] for h in range(4)],
               [Sres[h][0] for h in range(4)])
            E, Qf, R, Kf, Bt, BM = [Tl(fs[i][:, 0:T], fsres[i]) for i in range(6)]
            sets = [[Tl(mx[:, i, :], mxres[i]) for i in range(6)], [Tl(yb[:, i, :], ybres[i]) for i in range(6)]]
            SGs = [Tl(yb[:, 8, :], ybres[8]), Tl(yb[:, 9, :], ybres[9])]
            scm = [Tl(mx[:, 6, i * 128:(i + 1) * 128], Res("scm%d" % i)) for i in range(2)]
            alias([s.res for s in scm], [mxres[6]])
            osq = Tl(mx[:, 7, 0:512], mxres[7])
            O = [Tl(ps[5], psres[5]), Tl(ps[6], psres[6])]
            Bt3 = Bt.ap.rearrange("p (c j) -> p c j", j=64)
            BM3 = BM.ap.rearrange("p (c j) -> p c j", j=64)

            def projA(hd):
                if pre:
                    pan, pr = wpanel(win, 0, NKC, [(C_ZF + hd * 128, 128)])
                else:
                    pan, pr = wpanel(win, 0, NKC, [(C_ZF + hd * 128, 128), (C_HQ + hd * 128, 128)])
                panAs[hd] = (pan, pr)
                for half in range(2):
                    cs = slice(half * 512, (half + 1) * 512)
                    b = proj_h(pan, pr, 0, half)
                    act(E.ap[:, cs], b.ap[:, :], AF.Exp, [b.res], [E.res], scale=-1.0)
                    yield None

            def projB(hd):
                if pre:
                    return
                pan, pr = panAs[hd]
                for half in range(2):
                    cs = slice(half * 512, (half + 1) * 512)
                    b = proj_h(pan, pr, 128, half)
                    act(Qf.ap[:, cs], b.ap[:, :], AF.Silu, [b.res], [Qf.res])
                    yield None

            def elem(hd):
                par = hd % 2
                Q1, Q2, K2, K3, K3T, Vh = sets[par]
                SG = SGs[par]
                dect = dec[:, par, :]
                T1 = R
                if pre:
                    panB, prB = wpanel(win, 0, NKC, [(C_HV + hd * 128, 128)])
                    hvoff = 0
                else:
                    panB, prB = wpanel(win, 0, NKC, [(C_OG + hd * 128, 128), (C_HV + hd * 128, 128)])
                    hvoff = 128
                    for half in range(2):
                        cs = slice(half * 512, (half + 1) * 512)
                        b = proj_h(panB, prB, 0, half)
                        act(SG.ap[:, cs], b.ap[:, :], AF.Silu, [b.res], [SG.res])
                        yield None
                for g2 in range(2):
                    b = getps()
                    for t4 in range(4):
                        t8 = g2 * 4 + t4
                        mm(b.ap[:, t4 * 128:(t4 + 1) * 128], b.res,
                           [(hT[:, kc, t8 * 128:(t8 + 1) * 128], panB[:, kc, hvoff:hvoff + 128]) for kc in range(NKC)],
                           reads=[prB, hres[t8 // 4]])
                    act(Vh.ap[:, g2 * 512:(g2 + 1) * 512], b.ap[:, :], AF.Copy, [b.res], [Vh.res])
                    yield None
                act(R.ap, E.ap, AF.Ln, [E.res], [R.res], bias=1.0)
                yield None
                act(R.ap, R.ap, AF.Exp, [R.res], [R.res], scale=-1.0)
                yield None
                stt(Kf.ap, E.ap, lbp[:, 12 + hd:13 + hd], R.ap, ALU.mult, ALU.mult, [E.res, R.res, lbres], [Kf.res])
                yield "E_FREE"
                ts(R.ap, R.ap, lbp[:, 8 + hd:9 + hd], lbp[:, 4 + hd:5 + hd], ALU.mult, ALU.add, [R.res, lbres], [R.res])
                yield None
                act(R.ap, R.ap, AF.Ln, [R.res], [R.res])
                yield None
                dve(lambda e: e.tensor_tensor_scan(out=Bt.ap, data0=rmask[:, :], data1=R.ap, initial=0.0,
                                                   op0=ALU.mult, op1=ALU.add), [R.res, cres], [Bt.res])
                yield None
                if not pre:
                    tt(BM3, Bt3, Bt3[:, :, 31:32].to_broadcast([128, 16, 64]), ALU.subtract, [Bt.res], [BM.res])
                    yield None
                    act(T1.ap, BM.ap, AF.Exp, [BM.res], [T1.res])
                    yield None
                    tt(Q2.ap, Qf.ap, T1.ap, ALU.mult, [Qf.res, T1.res], [Q2.res])
                    yield None
                    act(T1.ap, BM.ap, AF.Exp, [BM.res], [T1.res], scale=-1.0)
                    yield None
                    tt(K2.ap, Kf.ap, T1.ap, ALU.mult, [Kf.res, T1.res], [K2.res])
                    yield None
                    act(T1.ap, Bt.ap, AF.Exp, [Bt.res], [T1.res])
                    yield None
                    tt(Q1.ap, Qf.ap, T1.ap, ALU.mult, [Qf.res, T1.res], [Q1.res])
                yield "Q_FREE"
                tt(BM3, Bt3, Bt3[:, :, 63:64].to_broadcast([128, 16, 64]), ALU.subtract, [Bt.res], [BM.res])
                yield None
                act(T1.ap, BM.ap, AF.Exp, [BM.res], [T1.res], scale=-1.0)
                yield None
                tt(K3.ap, Kf.ap, T1.ap, ALU.mult, [Kf.res, T1.res], [K3.res])
                yield None
                act(dect, Bt3[:, :, 63], AF.Exp, [Bt.res], [decres[par]])
                yield None
                for t8 in range(8):
                    P.op("pe", lambda e, t8=t8: e.transpose(psb[:, t8 * 128:(t8 + 1) * 128],
                                                            K3.ap[:, t8 * 128:(t8 + 1) * 128], cb(IDENT)),
                         reads=[K3.res, cres], writes=[psbres])
                act(K3T.ap, psb[:, :], AF.Copy, [psbres], [K3T.res])
                yield None

            def chunks(hd):
                par = hd % 2
                Q1, Q2, K2, K3, K3T, Vh = sets[par]
                SG = SGs[par]
                K3T3 = K3T.ap.rearrange("p (t c) -> p t c", c=128)
                Vh3 = Vh.ap.rearrange("p (t c) -> p t c", c=128)
                cur = 0
                for t8 in range(8):
                    tc_ = slice(t8 * 128, (t8 + 1) * 128)
                    Ob = O[t8 // 4]
                    oc0 = (t8 % 4) * 128
                    if not pre:
                        sc = getps()
                        mm(sc.ap[:, 0:128], sc.res, [(K2.ap[:, tc_], Q2.ap[:, tc_])], reads=[K2.res, Q2.res])
                        sm = scm[t8 % 2]
                        tt(sm.ap, sc.ap[:, 0:128], bdm[:, :], ALU.mult, [sc.res, cres], [sm.res])
                        P.op("pe", lambda e, Ob=Ob, oc0=oc0, t8=t8, sm=sm: e.matmul(
                            Ob.ap[:, oc0:oc0 + 128], Vh3[:, t8, :], sm.ap, start=True, stop=False),
                            reads=[Vh.res, sm.res], writes=[Ob.res])
                        yield
                    for cc in range(2):
                        c = 2 * t8 + cc
                        Sc = Sst[:, hd, cur, :]
                        if not pre:
                            P.op("pe", lambda e, Ob=Ob, oc0=oc0, cc=cc, c=c, Sc=Sc: e.matmul(
                                Ob.ap[:, oc0 + cc * 64:oc0 + cc * 64 + 64], Sc, Q1.ap[:, c * 64:(c + 1) * 64],
                                start=False, stop=True),
                                reads=[Sres[hd][cur], Q1.res], writes=[Ob.res])
                        su = getps()
                        p0 = cc * 64
                        mm(su.ap[:, 0:128], su.res, [(K3T3[p0:p0 + 64, t8, :], Vh3[p0:p0 + 64, t8, :])],
                           reads=[K3T.res, Vh.res])
                        Sn = Sst[:, hd, 1 - cur, :]
                        stt(Sn, Sc, dec[:, par, c:c + 1], su.ap[:, 0:128], ALU.mult, ALU.add,
                            [Sres[hd][cur], decres[par], su.res], [Sres[hd][1 - cur]])
                        cur = 1 - cur
                        yield
                assert cur == 0
                if pre:
                    return
                for half in range(2):
                    cs = slice(half * 512, (half + 1) * 512)
                    Ob = O[half]
                    act(osq.ap, Ob.ap[:, :], AF.Square, [Ob.res], [osq.res])
                    yield
                    b = getps()
                    mm(b.ap[:, :], b.res, [(cb(ONES), osq.ap)], reads=[osq.res, cres])
                    rs = Tl(rsn[:, :], rsnres)
                    act(rs.ap, b.ap[:, :], AF.Ln, [b.res, cres], [rs.res], scale=1.0 / 128.0, bias=epst[:, 0:1])
                    yield
                    act(rs.ap, rs.ap, AF.Exp, [rs.res], [rs.res], scale=-0.5)
                    yield
                    tt(rs.ap, Ob.ap[:, :], rs.ap, ALU.mult, [Ob.res, rs.res], [rs.res])
                    yield
                    stt(yb[:, 12 + hd, cs], SG.ap[:, cs], prm[:, L, 52 + hd:53 + hd], rs.ap, ALU.mult, ALU.mult,
                        [SG.res, rs.res, cres], [ybres[12 + hd]])
                    yield

            def run_step(C, El, PA, PB):
                e_free = El is None
                q_free = El is None
                live = {"C": C, "El": El, "PA": PA, "PB": PB}

                def adv(k):
                    g = live[k]
                    if g is None:
                        return None
                    try:
                        return next(g)
                    except StopIteration:
                        live[k] = None
                        return "DONE"
                while any(v is not None for v in live.values()):
                    adv("C")
                    tag = adv("El")
                    if tag == "E_FREE":
                        e_free = True
                    elif tag == "Q_FREE":
                        q_free = True
                    elif tag == "DONE":
                        e_free = q_free = True
                    if e_free:
                        adv("PA")
                    if q_free and live["PA"] is None:
                        adv("PB")

            panAs = {}
            run_step(None, None, projA(0), projB(0))
            run_step(None, elem(0), projA(1), projB(1))
            for hd in range(4):
                run_step(chunks(hd),
                         elem(hd + 1) if hd + 1 < 4 else None,
                         projA(hd + 2) if hd + 2 < 4 else None,
                         projB(hd + 2) if hd + 2 < 4 else None)
            unalias([mxres[6]], [s_.res for s_ in scm])
            P.dma("sp", "sexp", [lambda e: e.dma_start(out=ex.S.rearrange("h k v -> k h v"), in_=Sst[:, :, 0, :])],
                  reads=[Sres[h][0] for h in range(4)], writes=[ex.rS])

        def pool_stage(L, ex, rem, pre):
            win = W["w_in"][L]
            U = [Tl(fs[g], fsres[g]) for g in range(4)]
            tmp = [Tl(fs[4], fsres[4]), Tl(fs[5], fsres[5])]
            P.dma("sp", "hrem", [lambda e, g=g: e.dma_start(out=fs[g][:, 0:16], in_=rem.H[g * 128:(g + 1) * 128, :])
                                 for g in range(4)], reads=[rem.rH], writes=[u.res for u in U])
            for g in range(4):
                ts(U[g].ap[:, 0:16], U[g].ap[:, 0:16], flag[:, 0:1], None, ALU.mult, None, [U[g].res, cres], [U[g].res])
            if not pre:
                P.dma("pool", "pw", [lambda e: e.dma_start(out=poolw[:, :, :], in_=W["pool_w"][L].rearrange("g c d -> c g d"))],
                      writes=[pwres])
            for j in range(2):
                pan, pres = wpanel(win, 0, NKC, [(C_POOL + j * 256, 256)])
                for sub in range(2):
                    g = 2 * j + sub
                    for half in range(2):
                        b = proj_h(pan, pres, sub * 128, half)
                        act(U[g].ap[:, 16 + half * 512:16 + (half + 1) * 512], b.ap[:, :], AF.Copy, [b.res], [U[g].res])
            P.dma("sp", "hexp", [lambda e, g=g: e.dma_start(out=ex.H[g * 128:(g + 1) * 128, :], in_=fs[g][:, 1024:1040])
                                 for g in range(4)], reads=[u.res for u in U], writes=[ex.rH])
            if pre:
                return
            for g in range(4):
                w = 2 ** (g + 1)
                cur = U[g]
                for s in range(g + 1):
                    sh = 2 ** s
                    lo = 2 ** (s + 1) - 1
                    nxt = tmp[s % 2]
                    tt(nxt.ap[:, lo:1040], cur.ap[:, lo:1040], cur.ap[:, lo - sh:1040 - sh], ALU.add,
                       [cur.res], [nxt.res])
                    cur = nxt
                Wt = cur
                tt(fx[:, :], Wt.ap[:, 16:32], invc[:, g, :], ALU.mult, [Wt.res, cres], [fxres])
                ts(Wt.ap[:, 16:1040], Wt.ap[:, 16:1040], 1.0 / w, None, ALU.mult, None, [Wt.res], [Wt.res])
                dve(lambda e, Wt=Wt: e.tensor_copy(out=Wt.ap[:, 16:32], in_=fx[:, :]), [fxres, Wt.res], [Wt.res])
                Mb = Tl(mx[:, g % 2, :], mxres[g % 2])
                tt(Mb.ap, Wt.ap[:, 16:1040], U[g].ap[:, 16:1040], ALU.subtract, [Wt.res, U[g].res], [Mb.res])
                for half in range(2):
                    cs = slice(half * 512, (half + 1) * 512)
                    b = getps()
                    mm(b.ap[:, :], b.res, [(poolw[:, g, :], Mb.ap[:, cs])], reads=[pwres, Mb.res])
                    act(yb[:, g, cs], b.ap[:, :], AF.Copy, [b.res, cres], [ybres[g]], scale=prm[:, L, 48 + g:49 + g])

        def att_stage(L, ex, rem):
            win = W["w_in"][L]
            has_rem = rem is not zexp
            jlo = 0 if has_rem else 8
            kT = Tl(mx[:, 0:2, :].rearrange("p a t -> p (a t)"), None)
            kres = [mxres[0], mxres[1]]
            vt = mx[:, 2:4, :].rearrange("p a (j c) -> p (a j) c", c=128)
            vres = [mxres[2], mxres[3]]
            qT = Tl(mx[:, 4, :], mxres[4])
            spw = [Tl(mx[:, 5, :], mxres[5]), Tl(mx[:, 7, :], mxres[7])]
            Atw = Tl(mx[:, 6, :], mxres[6])
            etw = Tl(fs[0][:, 0:T], fsres[0])
            cf = [Tl(fs[2 + i][:, 0:T], fsres[2 + i]) for i in range(3)]
            cbsrc = [fs[1][:, 0:512].bitcast(BF16), fs[1][:, 512:1024].bitcast(BF16), fs[5][:, 0:512].bitcast(BF16)]
            cbres_ = [Res("cbf%d" % i) for i in range(3)]
            alias(cbres_[0:2], [fsres[1]])
            alias(cbres_[2:3], [fsres[5]])
            cbf = [Tl(cbsrc[i], cbres_[i]) for i in range(3)]
            O = [Tl(ps[5], psres[5]), Tl(ps[6], psres[6])]

            def pieces(jc):
                t0 = max(0, (jc - 8) * 128)
                out = []
                for half in range(2):
                    c0 = max(t0, half * 512)
                    c1 = (half + 1) * 512
                    if c0 < c1:
                        out.append((half, c0, c1))
                return out

            for h in range(8):
                pan, pres = wpanel(win, 0, NKC, [(C_SQ + h * 128, 128)])
                for half in range(2):
                    b = proj_h(pan, pres, 0, half)
                    act(qT.ap[:, half * 512:(half + 1) * 512], b.ap[:, :], AF.Copy, [b.res], [qT.res], scale=128.0 ** -0.5)
                kfn = [lambda e, h=h: e.dma_start(out=kT.ap[:, T:2 * T], in_=ex.K[h * 128:(h + 1) * 128, :])]
                vfn = [lambda e, h=h: e.dma_start(out=vt[:, 8:16, :], in_=ex.V[:, h * 128:(h + 1) * 128].rearrange("(j p) c -> p j c", p=128))]
                krd, vrd = [ex.rK[h]], [ex.rV[h // 2]]
                if has_rem:
                    kfn.append(lambda e, h=h: e.dma_start(out=kT.ap[:, 0:T], in_=rem.K[h * 128:(h + 1) * 128, :]))
                    vfn.append(lambda e, h=h: e.dma_start(out=vt[:, 0:8, :], in_=rem.V[:, h * 128:(h + 1) * 128].rearrange("(j p) c -> p j c", p=128)))
                    krd.append(rem.rK[h])
                    vrd.append(rem.rV[h // 2])
                P.dma("sp", "kl", kfn, reads=krd, writes=kres)
                P.dma("sp", "vl", vfn, reads=vrd, writes=vres)
                if has_rem:
                    ts(vt[:, 0:8, :], vt[:, 0:8, :], flag[:, 0:1], None, ALU.mult, None, vres + [cres], vres)
                for i in range(3):
                    dve(lambda e, i=i: e.memset(cf[i].ap, 0.0), [], [cf[i].res])
                    dve(lambda e, i=i: e.memset(cbf[i].ap, 0.0), [], [cbf[i].res])

                def t0_of(jc):
                    return max(0, (jc - 8) * 128)

                def halves_res(jc, par):
                    return [psres[2 * par + half] for (half, c0, c1) in pieces(jc)]

                def emit_Z(jc):
                    par = jc % 2
                    kc_ap = kT.ap[:, jc * 128:(jc + 1) * 128]
                    for (half, c0, c1) in pieces(jc):
                        mm(pair[par][:, c0:c1], psres[2 * par + half], [(kc_ap, qT.ap[:, c0:c1])], reads=kres + [qT.res])

                def emit_G(jc):
                    par = jc % 2
                    s_ = spw[par]
                    cbo = cbf[jc % 3]
                    for (half, c0, c1) in pieces(jc):
                        diag = (jc >= 8 and c0 == (jc - 8) * 128)

                        def fn(e, par=par, c0=c0, c1=c1, s_=s_, cbo=cbo, diag=diag):
                            if diag:
                                e.matmul(pair[par][:, c0:c0 + 128], cb(IDENT), cb(NEGTRI), start=False, stop=False)
                            e.matmul(pair[par][:, c0:c1], cb(NEGUI), s_.ap[:, c0:c1], start=False, stop=False)
                            return e.matmul(pair[par][:, c0:c1], cb(INV128), cbo.ap[:, c0:c1], start=False, stop=True)
                        P.op("pe", fn, reads=[s_.res, cbo.res, cres], writes=[psres[2 * par + half]])

                def emit_expE(jc):
                    par = jc % 2
                    t0 = t0_of(jc)
                    act(etw.ap[:, t0:T], pair[par][:, t0:T], AF.Exp, halves_res(jc, par), [etw.res])

                def emit_expA(jc):
                    par = jc % 2
                    t0 = t0_of(jc)
                    act(Atw.ap[:, t0:T], pair[par][:, t0:T], AF.Exp, halves_res(jc, par), [Atw.res])

                def emit_ln_tot(jc):
                    par = jc % 2
                    t0 = t0_of(jc)
                    s_ = spw[par]
                    act(s_.ap[:, t0:T], etw.ap[:, t0:T], AF.Ln, [etw.res], [s_.res], bias=1.0)
                    if jc >= 8:
                        tt(s_.ap[:, t0:t0 + 128], s_.ap[:, t0:t0 + 128], cb(TRI01), ALU.mult, [s_.res, cres], [s_.res])
                    if jc > jlo:
                        for (half, c0, c1) in pieces(jc):
                            wdt = c1 - c0
                            tot = Tl(ps[4] if half == 0 else ps[7], psres[4] if half == 0 else psres[7])
                            mm(tot.ap[:, 0:wdt], tot.res, [(cb(ONES), s_.ap[:, c0:c1])], reads=[s_.res, cres])
                            cn, co = cf[(jc - 1) % 3], cf[jc % 3]
                            tt(cn.ap[:, c0:c1], co.ap[:, c0:c1], tot.ap[:, 0:wdt], ALU.subtract,
                               [co.res, tot.res], [cn.res])
                            cbn = cbf[(jc - 1) % 3]
                            dve(lambda e, cbn=cbn, cn=cn, c0=c0, c1=c1: e.tensor_copy(out=cbn.ap[:, c0:c1], in_=cn.ap[:, c0:c1]),
                                [cn.res], [cbn.res])

                def emit_AV(jc):
                    for (half, c0, c1) in pieces(jc):
                        wdt = c1 - c0
                        last = (jc == jlo)
                        o0 = c0 - half * 512
                        P.op("pe", lambda e, half=half, o0=o0, wdt=wdt, c0=c0, c1=c1, jc=jc, last=last: e.matmul(
                            O[half].ap[:, o0:o0 + wdt], vt[:, jc, :], Atw.ap[:, c0:c1], start=False, stop=last),
                            reads=vres + [Atw.res], writes=[O[half].res])

                for half in range(2):
                    mm(O[half].ap[:, :], O[half].res, [(cb(ZEROS), qT.ap[:, half * 512:(half + 1) * 512])],
                       reads=[cres, qT.res])
                emit_Z(15)
                emit_expE(15)
                emit_ln_tot(15)
                for jc in range(15, jlo - 1, -1):
                    if jc > jlo:
                        emit_Z(jc - 1)
                    emit_G(jc)
                    if jc > jlo:
                        emit_expE(jc - 1)
                    emit_expA(jc)
                    emit_AV(jc)
                    if jc > jlo:
                        emit_ln_tot(jc - 1)
                for half in range(2):
                    act(yb[:, 4 + h, half * 512:(half + 1) * 512], O[half].ap[:, :], AF.Copy, [O[half].res], [ybres[4 + h]])
            unalias([fsres[1]], cbres_[0:2])
            unalias([fsres[5]], cbres_[2:3])

        def xadd(dc, half, b, reads_extra=(), feed=False):
            cs = slice(half * 512, (half + 1) * 512)
            tt(xT[:, dc, cs], xT[:, dc, cs], b.ap[:, :], ALU.add, [xres[dc][half], b.res] + list(reads_extra), [xres[dc][half]])
            if feed:
                ssq_feed(dc, half)

        def merge_stage(L):
            win = W["w_in"][L]
            brs = [(W["w_br_pool"][L], 4, 0), (W["w_br_sb"][L], 8, 4), (W["w_br_hg"][L], 4, 12)]
            acc = [[Tl(fs[sub][:, half * 512:(half + 1) * 512], Res("acc%d%d" % (sub, half))) for half in range(2)] for sub in range(2)]
            alias([acc[s][h].res for s in range(2) for h in range(2)], [fsres[0], fsres[1]])
            gs = [Tl(fs[2][:, i * 512:(i + 1) * 512], Res("gs%d" % i)) for i in range(2)]
            alias([g.res for g in gs], [fsres[2]])
            tm = [Tl(fs[3][:, i * 512:(i + 1) * 512], Res("tm%d" % i)) for i in range(2)]
            alias([g.res for g in tm], [fsres[3]])
            n = 0
            for grp in range(2):
                for dl in range(8):
                    dca = grp * 8 + dl
                    sub = dl % 2
                    for bi, (wbr, nkb, ybase) in enumerate(brs):
                        (gpan, bpan), pres_ = wmulti([(win, 0, NKC, C_GL + bi * D + dca * 128, 128),
                                                      (wbr, 0, nkb, dca * 128, 128)])
                        for half in range(2):
                            cs = slice(half * 512, (half + 1) * 512)
                            gb = proj_h(gpan, pres_, 0, half)
                            g_ = gs[n % 2]
                            t_ = tm[n % 2]
                            n += 1
                            act(g_.ap, gb.ap[:, :], AF.Sigmoid, [gb.res], [g_.res])
                            bb = getps()
                            mm(bb.ap[:, :], bb.res,
                               [(bpan[:, kc, :], yb[:, ybase + kc, cs]) for kc in range(nkb)],
                               reads=[pres_] + [ybres[ybase + kc] for kc in range(nkb)])
                            a_ = acc[sub][half]
                            if bi == 0:
                                tt(a_.ap, g_.ap, bb.ap[:, :], ALU.mult, [g_.res, bb.res], [a_.res])
                            else:
                                tt(t_.ap, g_.ap, bb.ap[:, :], ALU.mult, [g_.res, bb.res], [t_.res])
                                if bi == 1:
                                    tt(a_.ap, a_.ap, t_.ap, ALU.add, [a_.res, t_.res], [a_.res])
                                else:
                                    tt(mx[:, dl, cs], a_.ap, t_.ap, ALU.add, [a_.res, t_.res], [mxres[dl]])
                wo = W["w_out"][L]
                for op_ in range(8):
                    pan, pres = wpanel(wo, grp * 8, 8, [(op_ * 256, 256)])
                    for sub in range(2):
                        oc = op_ * 2 + sub
                        for half in range(2):
                            cs = slice(half * 512, (half + 1) * 512)
                            b = getps()
                            mm(b.ap[:, :], b.res, [(pan[:, kc, sub * 128:(sub + 1) * 128], mx[:, kc, cs]) for kc in range(8)],
                               reads=[pres] + mxres)
                            xadd(oc, half, b, feed=(grp == 1))
            unalias([fsres[0], fsres[1]], [acc[s_][h_].res for s_ in range(2) for h_ in range(2)])
            unalias([fsres[2]], [g_.res for g_ in gs])
            unalias([fsres[3]], [g_.res for g_ in tm])

        def ffn_stage(L):
            wgu = W["w_gate_up"][L]
            wd = W["w_down"][L]
            sl = [Tl(fs[i // 2][:, (i % 2) * 512:(i % 2 + 1) * 512], Res("sl%d" % i)) for i in range(4)]
            alias([s.res for s in sl], [fsres[0], fsres[1]])
            n = 0
            for fg in range(4):
                for fc in range(11):
                    c0 = fg * 1408 + fc * 128
                    (gp, up), pr_ = wmulti([(wgu, 0, NKC, c0, 128), (wgu, 0, NKC, DFF + c0, 128)])
                    for half in range(2):
                        cs = slice(half * 512, (half + 1) * 512)
                        gb = proj_h(gp, pr_, 0, half)
                        s_ = sl[n % 4]
                        n += 1
                        act(s_.ap, gb.ap[:, :], AF.Silu, [gb.res], [s_.res])
                        ub = proj_h(up, pr_, 0, half)
                        tt(yb[:, fc, cs], s_.ap, ub.ap[:, :], ALU.mult, [s_.res, ub.res], [ybres[fc]])
                for op_ in range(8):
                    pan, pres = wpanel(wd, fg * 11, 11, [(op_ * 256, 256)])
                    for sub in range(2):
                        oc = op_ * 2 + sub
                        for half in range(2):
                            cs = slice(half * 512, (half + 1) * 512)
                            b = getps()
                            mm(b.ap[:, :], b.res, [(pan[:, kc, sub * 128:(sub + 1) * 128], yb[:, kc, cs]) for kc in range(11)],
                               reads=[pres] + ybres[0:11])
                            xadd(oc, half, b, feed=(fg == 3))
            unalias([fsres[0], fsres[1]], [s_.res for s_ in sl])

        def ple_stage(L, p_src):
            pb = mx[:, 0:2, :]
            pbres = [mxres[0], mxres[1]]
            P.dma("pool", "pl", [lambda e: e.dma_start(out=pb, in_=p_src.rearrange("(k p) t -> p k t", p=128))], writes=pbres)
            sl = [Tl(fs[i // 2][:, (i % 2) * 512:(i % 2 + 1) * 512], Res("pl%d" % i)) for i in range(4)]
            alias([s.res for s in sl], [fsres[0], fsres[1]])
            n = 0
            for dc in range(16):
                (gp, pp), pr_ = wmulti([(W["w_ple_gate"][L], 0, NKC, dc * 128, 128), (W["w_ple_proj"][L], 0, 2, dc * 128, 128)])
                for half in range(2):
                    cs = slice(half * 512, (half + 1) * 512)
                    gb = proj_h(gp, pr_, 0, half)
                    s_ = sl[n % 4]
                    n += 1
                    act(s_.ap, gb.ap[:, :], AF.Sigmoid, [gb.res], [s_.res])
                    b = getps()
                    mm(b.ap[:, :], b.res, [(pp[:, kc, :], pb[:, kc, cs]) for kc in range(2)], reads=[pr_] + pbres)
                    tt(s_.ap, s_.ap, b.ap[:, :], ALU.mult, [s_.res, b.res], [s_.res])
                    tt(xT[:, dc, cs], xT[:, dc, cs], s_.ap, ALU.add, [xres[dc][half], s_.res], [xres[dc][half]])
                    ssq_feed(dc, half)
            unalias([fsres[0], fsres[1]], [s_.res for s_ in sl])

        def dump(nm, src_ap, res_list):
            if dbg and nm in dbgt:
                dst = dbgt.pop(nm)
                P.dma("sp", "dbg", [lambda e: e.dma_start(out=dst[:, :, :], in_=src_ap)], reads=res_list)

        allx = [xres[dc][h_] for dc in range(NKC) for h_ in range(2)]
        exps = {}
        for pi, ps_ in enumerate(passes):
            L = ps_["layer"]
            ex = mk_exp(str(pi))
            exps[ps_["name"]] = ex
            rem = zexp if ps_["remote"] is None else exps[ps_["remote"]]
            if ps_["x_src"] is not None:
                load_x({"own": x_own, "oth": x_oth}[ps_["x_src"]])
            norm_stage(lambda dc: prm[:, L, dc:dc + 1], pre=(ps_["x_src"] is None and ssq_state["valid"]))
            dump("dbg_h", hT[:, :, :], hres)
            kv_stage(L, ex)
            hg_stage(L, ex, rem, ps_["pre"])
            pool_stage(L, ex, rem, ps_["pre"])
            if ps_["pre"]:
                continue
            att_stage(L, ex, rem)
            dump("dbg_yb", yb[:, :, :], ybres)
            merge_stage(L)
            dump("dbg_xmix", xT[:, :, :], allx)
            norm_stage(lambda dc: prm[:, L, 16 + dc:17 + dc], pre=ssq_state["valid"])
            ffn_stage(L)
            dump("dbg_xffn", xT[:, :, :], allx)
            norm_stage(lambda dc: prm[:, L, 32 + dc:33 + dc], pre=ssq_state["valid"])
            ple_stage(L, {"own": p_own[L], "oth": p_oth}[ps_["p_src"]])
            dump("dbg_xple", xT[:, :, :], allx)
            if ps_.get("final"):
                norm_stage(lambda dc: normf[:, dc:dc + 1], to_out=out_d, pre=ssq_state["valid"])
        if "dbg" in P.dsem_cnt:
            P.wait_tok("sp", ("dbg", P.dsem_cnt["dbg"]))
        for q in range(4):
            if ("o%d" % q) in P.dsem_cnt:
                P.wait_tok("sp", ("o%d" % q, P.dsem_cnt["o%d" % q]))
        P.emit()
    return nc


PASSES_FUSED = [
    dict(name="p1", layer=0, x_src="oth", remote=None, pre=False, p_src="oth"),
    dict(name="p3", layer=1, x_src=None, remote=None, pre=True, p_src=None),
    dict(name="p2", layer=0, x_src="own", remote="p1", pre=False, p_src="own"),
    dict(name="p4", layer=1, x_src=None, remote="p3", pre=False, p_src="own", final=True),
]


def _consts():
    s = np.arange(128)[:, None]
    t = np.arange(128)[None, :]
    cb = np.zeros((128, 7, 128), np.float32)
    cb[:, 0, :] = (s < t)
    cb[:, 1, :] = np.where(s < t, 0.0, NEG)
    cb[:, 2, :] = np.where(s >= t, -1.0, 0.0)
    cb[:, 3, :] = (s == t)
    cb[:, 4, :] = 1.0
    cb[:, 6, :] = 1.0 / 128.0
    bdm = ((s // 64 == t // 64) & (s <= t)).astype(np.float32)
    rmask = np.ones((128, T), np.float32)
    rmask[:, ::64] = 0.0
    return cb, bdm, rmask


def _in_maps(inputs, NL=2):
    f = lambda a: np.ascontiguousarray(np.asarray(a, dtype=np.float32))
    x = f(inputs["x"])
    p = f(inputs["p"])
    cb, bdm, rmask = _consts()
    col = lambda v: np.ascontiguousarray(v.reshape(-1, 128).T)
    prm = np.zeros((128, 2, 64), np.float32)
    for L in range(2):
        prm[:, L, 0:16] = col(f(inputs["norm_mix"])[L])
        prm[:, L, 16:32] = col(f(inputs["norm_ffn"])[L])
        prm[:, L, 32:48] = col(f(inputs["norm_ple"])[L])
        prm[:, L, 48:52] = col(f(inputs["pool_scale"])[L])
        prm[:, L, 52:56] = col(f(inputs["hg_norm"])[L])
        prm[:, L, 56:60] = col(f(inputs["hg_lb"])[L])
    normf = col(f(inputs["norm_final"]))
    ws = {k: f(inputs[k])[0:NL] for k in WNAMES}
    zK = np.zeros((1024, T), ml_dtypes.bfloat16)
    zV = np.zeros((T, 1024), ml_dtypes.bfloat16)
    zS = np.zeros((4, 128, 128), ml_dtypes.bfloat16)
    zH = np.zeros((512, 16), np.float32)
    maps = []
    for c in range(8):
        b, role = c // 2, c % 2
        m = dict(ws)
        own = x[b, role * T:(role + 1) * T, :]
        m["x_own"] = np.ascontiguousarray(own.T)
        m["p_own"] = np.ascontiguousarray(p[:, b, role * T:(role + 1) * T, :].transpose(0, 2, 1))
        if role == 1:
            m["x_oth"] = np.ascontiguousarray(x[b, 0:T, :].T)
            m["p_oth"] = np.ascontiguousarray(p[0, b, 0:T, :].T)
        else:
            m["x_oth"] = np.zeros((D, T), np.float32)
            m["p_oth"] = np.zeros((256, T), np.float32)
        m["prm"] = prm
        m["normf"] = normf
        m["flag"] = np.full((128, 1), float(role), np.float32)
        m["cb"] = cb
        m["bdm"] = bdm
        m["rmask"] = rmask
        invc = np.zeros((128, 4, 16), np.float32)
        for g in range(4):
            w = 2 ** (g + 1)
            pos = np.arange(16) + role * T
            invc[:, g, :] = 1.0 / np.minimum(pos + 1, w)
        m["invc"] = invc
        m["zK"], m["zV"], m["zS"], m["zH"] = zK, zV, zS, zH
        maps.append(m)
    return maps


_NC_CACHE = {}


def kernel(**inputs):
    if "fused" not in _NC_CACHE:
        _NC_CACHE["fused"] = build(PASSES_FUSED)
    nc = _NC_CACHE["fused"]
    maps = _in_maps(inputs)
    res = run_bass_kernel_spmd(nc, maps, core_ids=list(range(8)))
    out = np.zeros((4, 2 * T, D), np.float32)
    for c in range(8):
        b, role = c // 2, c % 2
        out[b, role * T:(role + 1) * T, :] = np.asarray(res.results[c]["out"], dtype=np.float32).T
    return out
```
